# Optimizing a Trainium2 kernel written in Bass

```python
import jax, jax.numpy as jnp
from jax import lax
import numpy as np

D_MODEL = 1024
BATCH = 16
SEQ = 4096
DEPTH = 4
DEC_BATCH = 4
DEC_SEQ = 4096
PAST_LEN = 128

GRID_W = 64
GLA_HEADS = 4
GLA_DK = 64
GLA_DV = 128
GLA_KEY = GLA_HEADS * GLA_DK
GLA_VAL = GLA_HEADS * GLA_DV
GLA_RANK = 16
GLA_TAU = 16.0
GLA_CHUNK = 64
POOL_GROUPS = 4
POOL_GC = 64
POOL_WIDTH = POOL_GROUPS * POOL_GC
POOL_WINDOWS = (2, 4, 8, 16)
NA_HEADS = 4
NA_HD = 64
NA_WIDTH = NA_HEADS * NA_HD
NA_WIN_R = 8
NA_WIN_C = 16
N_BRANCH = 3
D_FF = -(-8 * D_MODEL // (3 * 256)) * 256
DEEPNORM_ALPHA = (2 * DEPTH) ** 0.25
DEEPNORM_BETA = (8 * DEPTH) ** -0.25
LN_EPS = 1e-5
RMS_EPS = 1e-6

IN_SIZES = (GLA_KEY, GLA_KEY, GLA_VAL, GLA_VAL, 2 * GLA_RANK, POOL_WIDTH, 3 * NA_WIDTH, N_BRANCH * D_MODEL)
D_IN = sum(IN_SIZES)
IN_OFFSETS = [int(v) for v in np.cumsum(IN_SIZES)[:-1]]

kernel_name = 'hybrid_gla_pool_natten_encoder'


def layer_norm(x, g, b):
    xf = x.astype(jnp.float32)
    mu = jnp.mean(xf, axis=-1, keepdims=True)
    var = jnp.mean(jnp.square(xf - mu), axis=-1, keepdims=True)
    return ((xf - mu) * lax.rsqrt(var + LN_EPS) * g + b).astype(x.dtype)


def gla_one_direction(q, k, v, log_a):
    B, T, H, dk = q.shape
    dv = v.shape[-1]
    L = GLA_CHUNK
    n = T // L
    q, k, log_a = (a.reshape(B, n, L, H, dk) for a in (q, k, log_a))
    v = v.reshape(B, n, L, H, dv)
    b = jnp.cumsum(log_a, axis=2)
    b_end = b[:, :, -1]
    q_dec = q * jnp.exp(b)
    k_inv = k * jnp.exp(-b)
    att = jnp.einsum('bnlhd,bnmhd->bnhlm', q_dec, k_inv)
    att = jnp.where(jnp.tril(jnp.ones((L, L), dtype=bool)), att, 0.0)
    o_intra = jnp.einsum('bnhlm,bnmhe->bnlhe', att, v)
    k_end = k * jnp.exp(b_end[:, :, None] - b)
    ds = jnp.einsum('bnlhd,bnlhe->nbhde', k_end, v)
    decay = jnp.transpose(jnp.exp(b_end), (1, 0, 2, 3))

    def step(s, inp):
        dcy, d_s = inp
        return dcy[..., None] * s + d_s, s

    _, s_prev = lax.scan(step, jnp.zeros((B, H, dk, dv), q.dtype), (decay, ds))
    o_inter = jnp.einsum('bnlhd,nbhde->bnlhe', q_dec, s_prev)
    return (o_intra + o_inter).reshape(B, T, H, dv)


def gla_mixer(q, k, v, ogate, lr, up_f, up_b, bias_f, bias_b, norm_g):
    B, T, _ = q.shape
    f32 = jnp.float32
    lr = lr.astype(f32)
    lr_f, lr_b = lr[..., :GLA_RANK], lr[..., GLA_RANK:]
    heads_k = lambda a: a.reshape(B, T, GLA_HEADS, GLA_DK)
    log_a_f = heads_k(jax.nn.log_sigmoid(lr_f @ up_f + bias_f) / GLA_TAU)
    log_a_b = heads_k(jax.nn.log_sigmoid(lr_b @ up_b + bias_b) / GLA_TAU)
    qh = heads_k(q.astype(f32)) * (GLA_DK ** -0.5)
    kh = heads_k(k.astype(f32))
    vh = v.astype(f32).reshape(B, T, GLA_HEADS, GLA_DV)
    o_f = gla_one_direction(qh, kh, vh, log_a_f)
    flip = lambda a: jnp.flip(a, axis=1)
    o_b = flip(gla_one_direction(flip(qh), flip(kh), flip(vh), flip(log_a_b)))
    o = o_f + o_b
    o = o * lax.rsqrt(jnp.mean(jnp.square(o), axis=-1, keepdims=True) + RMS_EPS) * norm_g
    return o.reshape(B, T, GLA_VAL) * jax.nn.silu(ogate.astype(f32))


def pool_mixer(p, pool_w, pool_scale):
    B, T, C = p.shape
    pf = p.astype(jnp.float32)
    csum = jnp.concatenate([jnp.zeros((B, 1, C), jnp.float32), jnp.cumsum(pf, axis=1)], axis=1)
    t = jnp.arange(T)
    outs = []
    for gi, w in enumerate(POOL_WINDOWS):
        sl = slice(gi * POOL_GC, (gi + 1) * POOL_GC)
        lo = jnp.clip(t - w // 2, 0, T)
        hi = jnp.clip(t + w - w // 2, 0, T)
        cnt = (hi - lo).astype(jnp.float32)[:, None]
        mean = (csum[:, hi, sl] - csum[:, lo, sl]) / cnt
        outs.append((mean - pf[..., sl]) @ pool_w[gi])
    return jnp.concatenate(outs, axis=-1) * pool_scale


def neighborhood_attention(q, k, v, rpb):
    B, T, _ = q.shape
    rows = T // GRID_W
    wr = min(NA_WIN_R, rows)
    f32 = jnp.float32
    grid = lambda a: a.astype(f32).reshape(B, rows, GRID_W, NA_HEADS, NA_HD)
    qg = grid(q) * (NA_HD ** -0.5)
    kg, vg = grid(k), grid(v)
    cols = np.arange(GRID_W)
    c0 = np.clip(cols - NA_WIN_C // 2, 0, GRID_W - NA_WIN_C)
    col_idx = c0[:, None] + np.arange(NA_WIN_C)[None, :]
    col_off = col_idx - cols[:, None] + (NA_WIN_C - 1)
    rpb_c = rpb[:, :, col_off]

    def row_block(r):
        r0 = jnp.clip(r - wr // 2, 0, rows - wr)
        kb = lax.dynamic_slice_in_dim(kg, r0, wr, axis=1)
        vb = lax.dynamic_slice_in_dim(vg, r0, wr, axis=1)
        ks = kb[:, :, col_idx]
        vs = vb[:, :, col_idx]
        qr = lax.dynamic_index_in_dim(qg, r, axis=1, keepdims=False)
        s = jnp.einsum('bqhd,brqjhd->bhqrj', qr, ks)
        row_off = r0 + jnp.arange(wr) - r + (NA_WIN_R - 1)
        bias = jnp.take(rpb_c, row_off, axis=1)
        s = s + jnp.transpose(bias, (0, 2, 1, 3))[None]
        pr = jax.nn.softmax(s.reshape(B, NA_HEADS, GRID_W, wr * NA_WIN_C), axis=-1).reshape(s.shape)
        return jnp.einsum('bhqrj,brqjhd->bqhd', pr, vs)

    out = lax.map(row_block, jnp.arange(rows))
    return jnp.transpose(out, (1, 0, 2, 3, 4)).reshape(B, T, NA_WIDTH)


def encoder_layer(x, w_in, gla_up_f, gla_up_b, gla_bias_f, gla_bias_b, gla_norm, pool_w, pool_scale,
                  na_rpb, w_br_a, w_br_b, w_br_c, w_out, ln1_g, ln1_b, w_gate, w_up, w_down, ln2_g, ln2_b):
    h = x @ w_in
    aq, ak, av, ag, alr, pin, cqkv, gates = jnp.split(h, IN_OFFSETS, axis=-1)
    ya = gla_mixer(aq, ak, av, ag, alr, gla_up_f, gla_up_b, gla_bias_f, gla_bias_b, gla_norm) @ w_br_a
    yb = pool_mixer(pin, pool_w, pool_scale) @ w_br_b
    cq, ck, cv = jnp.split(cqkv, 3, axis=-1)
    yc = neighborhood_attention(cq, ck, cv, na_rpb) @ w_br_c
    g = jax.nn.sigmoid(gates.astype(jnp.float32))
    m = (g[..., :D_MODEL] * ya + g[..., D_MODEL:2 * D_MODEL] * yb + g[..., 2 * D_MODEL:] * yc)
    x = layer_norm(DEEPNORM_ALPHA * x + (m @ w_out).astype(x.dtype), ln1_g, ln1_b)
    f = (jax.nn.silu(x @ w_gate) * (x @ w_up)) @ w_down
    x = layer_norm(DEEPNORM_ALPHA * x + f.astype(x.dtype), ln2_g, ln2_b)
    return x


def trunk(x, w_in, gla_up_f, gla_up_b, gla_bias_f, gla_bias_b, gla_norm, pool_w, pool_scale, na_rpb,
          w_br_a, w_br_b, w_br_c, w_out, ln1_g, ln1_b, w_gate, w_up, w_down, ln2_g, ln2_b):
    for l in range(DEPTH):
        x = encoder_layer(x, w_in[l], gla_up_f[l], gla_up_b[l], gla_bias_f[l], gla_bias_b[l], gla_norm[l],
                          pool_w[l], pool_scale[l], na_rpb[l], w_br_a[l], w_br_b[l], w_br_c[l], w_out[l],
                          ln1_g[l], ln1_b[l], w_gate[l], w_up[l], w_down[l], ln2_g[l], ln2_b[l])
    return x


def setup_inputs(seed: int = 0) -> dict:
    key = jax.random.key(seed)
    ks = jax.random.split(key, 24)
    f32 = jnp.float32
    nrm = lambda kk, shape, scale: jax.random.normal(kk, shape, f32) * scale
    D = D_MODEL
    return {
        'x_prompt': nrm(ks[0], (BATCH, SEQ, D), 1.0),
        'x_sample': nrm(ks[1], (DEC_BATCH, DEC_SEQ, D), 1.0),
        'w_in': nrm(ks[2], (DEPTH, D, D_IN), D ** -0.5),
        'gla_up_f': nrm(ks[3], (DEPTH, GLA_RANK, GLA_KEY), GLA_RANK ** -0.5),
        'gla_up_b': nrm(ks[4], (DEPTH, GLA_RANK, GLA_KEY), GLA_RANK ** -0.5),
        'gla_bias_f': nrm(ks[5], (DEPTH, GLA_KEY), 0.5),
        'gla_bias_b': nrm(ks[6], (DEPTH, GLA_KEY), 0.5),
        'gla_norm': 1.0 + nrm(ks[7], (DEPTH, GLA_DV), 0.02),
        'pool_w': nrm(ks[8], (DEPTH, POOL_GROUPS, POOL_GC, POOL_GC), POOL_GC ** -0.5),
        'pool_scale': 1.0 + nrm(ks[9], (DEPTH, POOL_WIDTH), 0.02),
        'na_rpb': nrm(ks[10], (DEPTH, NA_HEADS, 2 * NA_WIN_R - 1, 2 * NA_WIN_C - 1), 0.1),
        'w_br_a': nrm(ks[11], (DEPTH, GLA_VAL, D), GLA_VAL ** -0.5 * DEEPNORM_BETA),
        'w_br_b': nrm(ks[12], (DEPTH, POOL_WIDTH, D), POOL_WIDTH ** -0.5 * DEEPNORM_BETA),
        'w_br_c': nrm(ks[13], (DEPTH, NA_WIDTH, D), NA_WIDTH ** -0.5 * DEEPNORM_BETA),
        'w_out': nrm(ks[14], (DEPTH, D, D), D ** -0.5 * DEEPNORM_BETA),
        'ln1_g': 1.0 + nrm(ks[15], (DEPTH, D), 0.02),
        'ln1_b': nrm(ks[16], (DEPTH, D), 0.02),
        'w_gate': nrm(ks[17], (DEPTH, D, D_FF), D ** -0.5),
        'w_up': nrm(ks[18], (DEPTH, D, D_FF), D ** -0.5 * DEEPNORM_BETA),
        'w_down': nrm(ks[19], (DEPTH, D_FF, D), D_FF ** -0.5 * DEEPNORM_BETA),
        'ln2_g': 1.0 + nrm(ks[20], (DEPTH, D), 0.02),
        'ln2_b': nrm(ks[21], (DEPTH, D), 0.02),
    }


def reference(x_prompt, x_sample, w_in, gla_up_f, gla_up_b, gla_bias_f, gla_bias_b, gla_norm, pool_w,
              pool_scale, na_rpb, w_br_a, w_br_b, w_br_c, w_out, ln1_g, ln1_b, w_gate, w_up, w_down,
              ln2_g, ln2_b):
    y_prompt = trunk(x_prompt, w_in, gla_up_f, gla_up_b, gla_bias_f, gla_bias_b, gla_norm, pool_w,
                     pool_scale, na_rpb, w_br_a, w_br_b, w_br_c, w_out, ln1_g, ln1_b, w_gate, w_up,
                     w_down, ln2_g, ln2_b)
    y_sample = trunk(x_sample, w_in, gla_up_f, gla_up_b, gla_bias_f, gla_bias_b, gla_norm, pool_w,
                     pool_scale, na_rpb, w_br_a, w_br_b, w_br_c, w_out, ln1_g, ln1_b, w_gate, w_up,
                     w_down, ln2_g, ln2_b)
    return (y_prompt, y_sample)
```

```python
import numpy as np
from contextlib import ExitStack
import concourse.bass as bass
import concourse.mybir as mybir
from concourse.bass_utils import run_bass_kernel_spmd

F32 = mybir.dt.float32
BF16 = mybir.dt.bfloat16
AF = mybir.ActivationFunctionType
ALU = mybir.AluOpType
AX = mybir.AxisListType

D = 1024
DFF = 2816
DIN = 5664
NMIX = 2592
ALPHA = 8.0 ** 0.25
LN_EPS = 1e-5
RMS_EPS = 1e-6
EPOCH = 12000
NEG = -30000.0


class Tok:
    __slots__ = ("eng", "sig", "dma")

    def __init__(self, eng, dma=False):
        self.eng = eng
        self.sig = None
        self.dma = dma


class Sched:
    def __init__(self, nc):
        self.nc = nc
        self.engs = {"pe": nc.tensor, "act": nc.scalar, "dve": nc.vector, "pool": nc.gpsimd, "sp": nc.sync}
        self.sems = {}
        self.cnt = {e: 0 for e in ("pe", "act", "dve", "pool")}
        self.dcnt = {}
        self.lastw = {}
        self.readers = {}
        self.waited = {e: {} for e in self.engs}
        self.pe_pending = []
        self.last_tok = {}
        self.barc = 0
        self.nins = 0

    def sem(self, name):
        if name not in self.sems:
            self.sems[name] = self.nc.alloc_semaphore(name)
        return self.sems[name]

    def _wait(self, eng, tok):
        if tok.sig is None:
            raise RuntimeError("dependency on unsignalled PE op")
        name, val = tok.sig
        if self.waited[eng].get(name, 0) >= val:
            return
        self.waited[eng][name] = val
        self.engs[eng].wait_ge(self.sem(name), val)
        self.nins += 1

    def op(self, eng, fn, r=(), w=(), sig=True, dsem=None):
        deps = []
        for k in r:
            t = self.lastw.get(k)
            if t is not None:
                deps.append(t)
        for k in w:
            t = self.lastw.get(k)
            if t is not None:
                deps.append(t)
            rd = self.readers.get(k)
            if rd:
                for e, v in rd.items():
                    if e == "dma":
                        deps.extend(v)
                    else:
                        deps.append(v)
        is_dma = dsem is not None
        best = {}
        for t in deps:
            if t.eng == "pe" and eng == "pe" and not is_dma and not t.dma:
                continue
            if t.sig is None:
                raise RuntimeError("dependency on unsignalled PE op")
            nm, v = t.sig
            if nm not in best or best[nm].sig[1] < v:
                best[nm] = t
        for t in best.values():
            self._wait(eng, t)
        ins = fn(self.engs[eng])
        self.nins += 1
        tok = Tok(eng, dma=is_dma)
        if is_dma:
            c = self.dcnt.get(dsem, 0) + 16
            self.dcnt[dsem] = c
            ins.then_inc(self.sem(dsem), 16)
            tok.sig = (dsem, c)
        elif eng == "pe" and not sig:
            self.pe_pending.append(tok)
        else:
            c = self.cnt[eng]
            ep, v = divmod(c, EPOCH)
            self.cnt[eng] = c + 1
            name = f"{eng}{ep}"
            ins.then_inc(self.sem(name), 1)
            tok.sig = (name, v + 1)
            self.last_tok[eng] = tok
            if eng == "pe":
                for p in self.pe_pending:
                    p.sig = tok.sig
                self.pe_pending = []
        for k in r:
            d = self.readers.setdefault(k, {})
            if is_dma:
                d.setdefault("dma", []).append(tok)
            else:
                d[eng] = tok
        for k in w:
            self.lastw[k] = tok
            self.readers[k] = {}
        return tok

    def barrier(self):
        assert not self.pe_pending
        sp = "sp"
        for e, t in self.last_tok.items():
            self._wait(sp, t)
        for name, c in self.dcnt.items():
            if self.waited[sp].get(name, 0) < c:
                self.waited[sp][name] = c
                self.engs[sp].wait_ge(self.sem(name), c)
        self.barc += 1
        self.engs[sp].nop().then_inc(self.sem("bar"), 1)
        for e in ("pe", "act", "dve", "pool"):
            self.engs[e].wait_ge(self.sem("bar"), self.barc)
        self.lastw = {}
        self.readers = {}
        self.nins += 6


def _consts(T):
    c = {}
    c["ident"] = np.eye(128, dtype=np.float32)
    m = np.arange(128)[:, None]
    l = np.arange(128)[None, :]
    same = (m // 64) == (l // 64)
    c["trix_f"] = (same & (m <= l)).astype(np.float32) / 16.0
    c["tris_f"] = (same & (m > l)).astype(np.float32) / 16.0
    c["trix_b"] = (same & (m >= l)).astype(np.float32) / 16.0
    c["tris_b"] = (same & (m < l)).astype(np.float32) / 16.0
    c["mask_f"] = np.repeat((same & (m <= l)).astype(np.float32)[:, None, :], 4, axis=1).copy()
    c["mask_b"] = np.repeat((same & (m >= l)).astype(np.float32)[:, None, :], 4, axis=1).copy()
    Tv = 384
    bm = np.zeros((5, 4, 128, 128), np.float32)
    for g, wd in enumerate((2, 4, 8, 16)):
        A = np.zeros((Tv, Tv), np.float32)
        for t in range(Tv):
            lo = min(max(t - wd // 2, 0), Tv)
            hi = min(max(t + wd - wd // 2, 0), Tv)
            A[t, lo:hi] = 1.0 / float(hi - lo)
            A[t, t] -= 1.0
        bm[0, g] = A[128:256, 0:128].T
        bm[1, g] = A[128:256, 128:256].T
        bm[2, g] = A[128:256, 256:384].T
        bm[3, g] = A[0:128, 0:128].T
        bm[4, g] = A[256:384, 256:384].T
    c["bmat"] = np.ascontiguousarray(bm.transpose(2, 0, 1, 3))
    c["negh"] = np.full((128, 4), -0.5, np.float32)
    return c


def _na_tiles(NB):
    def chunks(i):
        if i < 2:
            return [0, 1, 2, 3]
        if i > NB - 3:
            return [NB - 4, NB - 3, NB - 2, NB - 1]
        return [i - 2, i - 1, i, i + 1, i + 2]

    def tile_base(i):
        if i == 0:
            return 5
        if i == 1:
            return 9
        if i == NB - 2:
            return 13
        if i == NB - 1:
            return 17
        return 0
    return chunks, tile_base


def _na_bias_index(NB):
    rows = NB * 2
    chunks, tile_base = _na_tiles(NB)
    ri = np.zeros((21, 128, 128), np.int64)
    ci = np.zeros((21, 128, 128), np.int64)
    va = np.zeros((21, 128, 128), bool)
    kk = np.arange(128)[:, None]
    qq = np.arange(128)[None, :]
    for i in (2, 0, 1, NB - 2, NB - 1):
        for n, j in enumerate(chunks(i)):
            t = tile_base(i) + n
            kr = 2 * j + kk // 64
            kc = kk % 64
            qr = 2 * i + qq // 64
            qc = qq % 64
            r0 = np.clip(qr - 4, 0, rows - 8)
            c0 = np.clip(qc - 8, 0, 48)
            v = (kr >= r0) & (kr < r0 + 8) & (kc >= c0) & (kc < c0 + 16)
            ri[t] = np.where(v, kr - qr + 7, 0)
            ci[t] = np.where(v, kc - qc + 15, 0)
            va[t] = v
    return ri, ci, va


def build(NS, T, L):
    NT = NS * T
    NB = T // 128
    NTL = T // 512
    NTILE = NT // 512
    nc = bass.Bass("TRN2", target_bir_lowering=False)

    def din(name, shape, dt=F32):
        return nc.dram_tensor(name, list(shape), dt, kind="ExternalInput").ap()

    def dscr(name, shape, dt):
        return nc.dram_tensor(name, list(shape), dt).ap()

    x_in = din("x", [NT, D])
    y_out = nc.dram_tensor("y", [NT, D], F32, kind="ExternalOutput").ap()
    w_in = din("w_in", [L, D, DIN])
    w_bra = din("w_br_a", [L, 512, D])
    w_brb = din("w_br_b", [L, 256, D])
    w_brc = din("w_br_c", [L, 256, D])
    w_out = din("w_out", [L, D, D])
    w_gate = din("w_gate", [L, D, DFF])
    w_up = din("w_up", [L, D, DFF])
    w_down = din("w_down", [L, DFF, D])
    upx = din("upx", [L, 2, 17, 256])
    gnorm = din("gla_norm", [L, 128])
    pool_w = din("pool_w", [L, 4, 64, 64])
    pool_sc = din("pool_scale", [L, 128, 2])
    nabias = din("nabias", [L, 128, 21, 4, 128])
    lnp = din("lnp", [L, 4, D])
    c_ident = din("ident", [128, 128])
    c_tri = din("tri", [4, 128, 128])
    c_mask = din("mask", [2, 128, 4, 128])
    c_bmat = din("bmat", [128, 5, 4, 128])
    c_negh = din("negh", [128, 4])

    X1 = dscr("X1", [NT, D], F32)
    X2 = dscr("X2", [NT, D], F32)
    s_qT = dscr("s_qT", [NS, 256, T], BF16)
    s_kT = dscr("s_kT", [NS, 256, T], BF16)
    s_lrT = dscr("s_lrT", [NS, 32, T], F32)
    s_nqT = dscr("s_nqT", [NS, 256, T], BF16)
    s_nkT = dscr("s_nkT", [NS, 256, T], BF16)
    s_k = dscr("s_k", [NT, 256], BF16)
    s_v = dscr("s_v", [NT, 512], BF16)
    s_og = dscr("s_og", [NT, 512], BF16)
    s_p = dscr("s_p", [NT, 256], BF16)
    s_nv = dscr("s_nv", [NT, 260], BF16)
    s_gT = dscr("s_gT", [NS, 512, T], BF16)
    s_pbT = dscr("s_pbT", [NS, 256, T], BF16)
    s_naT = dscr("s_naT", [NS, 256, T], BF16)
    s_of = dscr("s_of", [NT, 512], F32)

    S = Sched(nc)
    op = S.op
    glob = ExitStack()

    uid = {"n": 0}

    def sbt(st, name, shape, dt):
        uid["n"] += 1
        return st.enter_context(nc.sbuf_tensor(f"sb{uid['n']}_{name}", list(shape), dt))

    PS = [glob.enter_context(nc.psum_tensor(f"psb{i}", [128, 512], F32)) for i in range(8)]
    ident = sbt(glob, "ident", [128, 128], BF16)
    op("pool", lambda e: e.dma_start(out=ident[:], in_=c_ident), w=["ident"], dsem="c_const")

    rr = {"ev": 0, "bank": 0}

    def evac(out_ap, in_ap, r, w, func=None, scale=None, eng=None):
        if func is not None or scale is not None:
            eng = "act"
        if eng is None:
            eng = "act" if rr["ev"] % 2 == 0 else "dve"
            rr["ev"] += 1
        if eng == "act":
            f = func if func is not None else AF.Copy
            if scale is not None:
                return op("act", lambda e: e.activation(out=out_ap, in_=in_ap, func=f, scale=scale), r=r, w=w)
            return op("act", lambda e: e.activation(out=out_ap, in_=in_ap, func=f), r=r, w=w)
        return op(eng, lambda e: e.tensor_copy(out=out_ap, in_=in_ap), r=r, w=w)

    def mm(out_ap, lhsT, rhs, start, stop, r, w, sig=None, skip=False):
        if sig is None:
            sig = stop
        if skip:
            return op("pe", lambda e: e.matmul(out_ap, lhsT, rhs, start=start, stop=stop, skip_group_check=True),
                      r=r, w=w, sig=sig)
        return op("pe", lambda e: e.matmul(out_ap, lhsT, rhs, start=start, stop=stop), r=r, w=w, sig=sig)

    def load_w(dst, src_rows, kchunks, dsem, key):
        for k in range(kchunks):
            op("pool", lambda e, k=k: e.dma_start(out=dst[:, k, :], in_=src_rows[k * 128:(k + 1) * 128, :]),
               w=[key], dsem=dsem)

    def make_xT(xblk_ap, xkey, xbf, xbfkey, xT, xTkey, b, bank, ceng="pool"):
        if ceng is None:
            pass
        elif ceng == "act":
            op("act", lambda e: e.activation(out=xbf[:], in_=xblk_ap, func=AF.Copy), r=[xkey], w=[xbfkey])
        else:
            op("pool", lambda e: e.tensor_copy(out=xbf[:], in_=xblk_ap), r=[xkey], w=[xbfkey])
        pk = f"ps{bank}"
        pv = PS[bank][:].bitcast(BF16).rearrange("p (k t) -> p k t", k=8)
        for k in range(8):
            op("pe", lambda e, k=k: e.transpose(pv[:, k, :], xbf[:, k * 128:(k + 1) * 128], ident[:]),
               r=[xbfkey, "ident"], w=[pk], sig=(k == 7))
        evac(xT[:, :, b * 128:(b + 1) * 128], pv, r=[pk], w=[xTkey])

    def nextbank(n=8, base=0):
        b = base + rr["bank"] % n
        rr["bank"] += 1
        return b

    def layer_norm(st_, xb, xkey, g_t, b_t, tagp, stats, mv, rstd):
        op("dve", lambda e: e.bn_stats(out=stats[:, 0:6], in_=xb[:, 0:512]), r=[xkey], w=[tagp + "st"])
        op("dve", lambda e: e.bn_stats(out=stats[:, 6:12], in_=xb[:, 512:1024]), r=[xkey], w=[tagp + "st"])
        yield
        op("dve", lambda e: e.bn_aggr(out=mv[:, 0:2], in_=stats[:, 0:12]), r=[tagp + "st"], w=[tagp + "mv"])
        op("dve", lambda e: e.tensor_scalar(out=mv[:, 2:3], in0=mv[:, 1:2], scalar1=LN_EPS, scalar2=None,
                                            op0=ALU.add), r=[tagp + "mv"], w=[tagp + "mv2"])
        yield
        op("pool", lambda e: e.tensor_tensor(out=rstd[:, 0:1], in0=mv[:, 2:3], in1=negh[:, 0:1], op=ALU.pow),
           r=[tagp + "mv2", "negh"], w=[tagp + "rs"])
        yield
        op("dve", lambda e: e.tensor_scalar(out=mv[:, 3:4], in0=mv[:, 0:1], scalar1=rstd[:, 0:1], scalar2=-1.0,
                                            op0=ALU.mult, op1=ALU.mult), r=[tagp + "mv", tagp + "rs"], w=[tagp + "nm"])
        yield
        op("act", lambda e: e.activation(out=xb, in_=xb, func=AF.Identity, bias=mv[:, 3:4], scale=rstd[:, 0:1]),
           r=[xkey, tagp + "nm", tagp + "rs"], w=[xkey])
        yield
        op("dve", lambda e: e.tensor_tensor(out=xb, in0=xb, in1=g_t[:], op=ALU.mult), r=[xkey, "lnp"], w=[xkey])
        yield
        op("pool", lambda e: e.tensor_tensor(out=xb, in0=xb, in1=b_t[:], op=ALU.add), r=[xkey, "lnp"], w=[xkey])
        yield

    def run_rr(gens):
        gens = list(gens)
        while gens:
            gens = [g for g in gens if next(g, "done") != "done"]

    class Deferred:
        def __init__(self):
            self.gens = []
            self.after = None

        def set(self, gens, after=None):
            self.gens = list(gens)
            self.after = after

        def advance(self, n):
            for _ in range(n):
                if not self.gens:
                    break
                g = self.gens.pop(0)
                if next(g, "done") != "done":
                    self.gens.append(g)
            if not self.gens and self.after is not None:
                f, self.after = self.after, None
                f()

        def drain(self):
            while self.gens or self.after is not None:
                self.advance(64)

    negh = sbt(glob, "negh", [128, 4], F32)
    op("sp", lambda e: e.dma_start(out=negh[:], in_=c_negh), w=["negh"], dsem="c_negh")

    def phase_A(l, xsrc):
        with ExitStack() as st:
            W = sbt(st, "A_w", [128, 8, NMIX], BF16)
            load_w(W, w_in[l][:, 0:NMIX], 8, "A_w", "A_w")
            xblk = [sbt(st, f"A_x{i}", [128, D], F32) for i in range(8)]
            xbf = [sbt(st, f"A_xbf{i}", [128, D], BF16) for i in range(4)]
            xT = [sbt(st, f"A_xT{i}", [128, 8, 512], BF16) for i in range(2)]
            stF = [sbt(st, f"A_stF{i}", [128, 8, 512], BF16) for i in range(2)]
            stL = [sbt(st, f"A_stL{i}", [32, 512], F32) for i in range(2)]
            stT = [sbt(st, f"A_stT{i}", [128, 4, 1536], BF16) for i in range(2)]
            stNV = [sbt(st, f"A_stNV{i}", [128, 4, 4, 65], BF16) for i in range(2)]
            for i in range(2):
                op("pool", lambda e, i=i: e.memset(stNV[i][:], 1.0), w=[f"A_stNV{i}"])
            FMG = [(0, 128, 0.125, s_qT, 0), (128, 128, 0.125, s_qT, 128), (256, 128, None, s_kT, 0),
                   (384, 128, None, s_kT, 128), (1824, 128, 0.125, s_nqT, 0), (1952, 128, 0.125, s_nqT, 128),
                   (2080, 128, None, s_nkT, 0), (2208, 128, None, s_nkT, 128)]
            TMG = [(256, 256, 0, None), (512, 512, 256, None), (1024, 512, 768, AF.Silu), (1568, 256, 1280, None)]

            def loads(ti):
                par = ti % 2
                for b in range(4):
                    i = par * 4 + b
                    op("sp", lambda e, i=i, b=b: e.dma_start(out=xblk[i][:], in_=xsrc[ti * 512 + b * 128: ti * 512 + (b + 1) * 128, :]),
                       w=[f"A_x{i}"], dsem=f"A_x{i}")

            def mk_xT(tn):
                pn = tn % 2
                for b in range(4):
                    op("act", lambda e, b=b: e.activation(out=xbf[b][:], in_=xblk[pn * 4 + b][:], func=AF.Copy),
                       r=[f"A_x{pn * 4 + b}"], w=[f"A_xbf{b}"])
                for b in range(4):
                    make_xT(None, None, xbf[b], f"A_xbf{b}", xT[pn], f"A_xT{pn}", b, nextbank(), ceng=None)

            loads(0)
            mk_xT(0)
            for ti in range(NTILE):
                par = ti % 2
                s, t0 = divmod(ti, NTL)
                t0 *= 512
                if ti + 1 < NTILE:
                    loads(ti + 1)
                xk = f"A_xT{par}"
                for gi, (c0, ncol, scale, scr, r0) in enumerate(FMG):
                    bk = nextbank()
                    for k in range(8):
                        mm(PS[bk][0:ncol, :], W[:, k, c0:c0 + ncol], xT[par][:, k, :], k == 0, k == 7,
                           r=["A_w", xk], w=[f"ps{bk}"])
                    evac(stF[par][0:ncol, gi, :], PS[bk][0:ncol, :], r=[f"ps{bk}"], w=[f"A_stF{par}"], scale=scale)
                if ti + 1 < NTILE:
                    mk_xT(ti + 1)
                bk = nextbank()
                for k in range(8):
                    mm(PS[bk][0:32, :], W[:, k, 1536:1568], xT[par][:, k, :], k == 0, k == 7, r=["A_w", xk], w=[f"ps{bk}"])
                evac(stL[par][:], PS[bk][0:32, :], r=[f"ps{bk}"], w=[f"A_stL{par}"])
                for b in range(4):
                    for (c0, ncol, off, func) in TMG:
                        bk = nextbank()
                        for k in range(8):
                            mm(PS[bk][:, 0:ncol], xT[par][:, k, b * 128:(b + 1) * 128], W[:, k, c0:c0 + ncol], k == 0, k == 7,
                               r=["A_w", xk], w=[f"ps{bk}"])
                        evac(stT[par][:, b, off:off + ncol], PS[bk][:, 0:ncol], r=[f"ps{bk}"], w=[f"A_stT{par}"], func=func)
                    bk = nextbank()
                    for k in range(8):
                        mm(PS[bk][:, 0:256], xT[par][:, k, b * 128:(b + 1) * 128], W[:, k, 2336:2592], k == 0, k == 7,
                           r=["A_w", xk], w=[f"ps{bk}"])
                    evac(stNV[par][:, b, :, 0:64], PS[bk][:, 0:256].rearrange("p (h d) -> p h d", h=4),
                         r=[f"ps{bk}"], w=[f"A_stNV{par}"])
                for gi, (c0, ncol, scale, scr, r0) in enumerate(FMG):
                    op("sp", lambda e, gi=gi, scr=scr, r0=r0: e.dma_start(out=scr[s, r0:r0 + 128, t0:t0 + 512], in_=stF[par][:, gi, :]),
                       r=[f"A_stF{par}"], dsem=f"A_stF{par}")
                op("sp", lambda e: e.dma_start(out=s_lrT[s, :, t0:t0 + 512], in_=stL[par][:]), r=[f"A_stL{par}"], dsem=f"A_stL{par}")
                rows = slice(ti * 512, ti * 512 + 512)
                for (scr, off, ncol) in ((s_k, 0, 256), (s_v, 256, 512), (s_og, 768, 512), (s_p, 1280, 256)):
                    op("sp", lambda e, scr=scr, off=off, ncol=ncol: e.dma_start(
                        out=scr[rows, :].rearrange("(b p) c -> p b c", p=128), in_=stT[par][:, :, off:off + ncol]),
                       r=[f"A_stT{par}"], dsem=f"A_stT{par}")
                op("sp", lambda e: e.dma_start(out=s_nv[rows, :].rearrange("(b p) c -> p b c", p=128),
                                               in_=stNV[par][:].rearrange("p b h d -> p b (h d)")),
                   r=[f"A_stNV{par}"], dsem=f"A_stNV{par}")
            S.barrier()

    def phase_B(l):
        chunks_of, tile_base = _na_tiles(NB)
        with ExitStack() as st:
            tri = sbt(st, "B_cst", [128, 4, 128], F32)
            for i in range(4):
                op("sp", lambda e, i=i: e.dma_start(out=tri[:, i, :], in_=c_tri[i]), w=["B_cst"], dsem="B_cst")
            msk = sbt(st, "B_cst", [128, 2, 4, 128], F32)
            for i in range(2):
                op("sp", lambda e, i=i: e.dma_start(out=msk[:, i], in_=c_mask[i]), w=["B_cst"], dsem="B_cst")
            upw = sbt(st, "B_cst", [17, 2, 256], F32)
            for i in range(2):
                op("sp", lambda e, i=i: e.dma_start(out=upw[:, i, :], in_=upx[l, i]), w=["B_cst"], dsem="B_cst")
            gnb = sbt(st, "B_cst", [128, 128], F32)
            op("sp", lambda e: e.dma_start(out=gnb[:], in_=gnorm[l].partition_broadcast(128)), w=["B_cst"], dsem="B_cst")
            bmat = sbt(st, "B_bmat", [128, 5, 4, 128], BF16)
            op("pool", lambda e: e.dma_start(out=bmat[:], in_=c_bmat), w=["B_bmat"], dsem="B_bmat")
            pwbd = sbt(st, "B_pwbd", [128, 2, 128], F32)
            pwbf = sbt(st, "B_pwbf", [128, 2, 128], BF16)
            op("pool", lambda e: e.memset(pwbd[:], 0.0), w=["B_pwbd"])
            for g in range(4):
                pb = (g % 2) * 64
                op("sp", lambda e, g=g, pb=pb: e.dma_start(out=pwbd[pb:pb + 64, g // 2, pb:pb + 64], in_=pool_w[l, g]),
                   w=["B_pwbd"], r=[], dsem="B_pwbd")
            op("pool", lambda e: e.tensor_copy(out=pwbf[:], in_=pwbd[:]), r=["B_pwbd"], w=["B_pwbf"])
            psc = sbt(st, "B_cst", [128, 2], F32)
            op("sp", lambda e: e.dma_start(out=psc[:], in_=pool_sc[l]), w=["B_cst"], dsem="B_cst")
            nab = sbt(st, "B_nab", [128, 21, 4, 128], BF16)
            oft = [sbt(st, f"B_of{i}", [128, 4, 512], F32) for i in range(2)]
            nq = sbt(st, "B_nq", [128, 2, T], BF16)
            nk = sbt(st, "B_nk", [128, 2, T], BF16)
            nvx = sbt(st, "B_nvx", [128, NB, 260], BF16)
            pT = nq[:].rearrange("p a t -> p (a t)").rearrange("p (b c) -> p b c", c=256)
            qTt = [sbt(st, f"B_qT{i}", [128, 2, 512], BF16) for i in range(2)]
            kTt = [sbt(st, f"B_kT{i}", [128, 2, 512], BF16) for i in range(2)]
            kt = [sbt(st, f"B_k{i}", [128, 4, 256], BF16) for i in range(2)]
            vt = [sbt(st, f"B_v{i}", [128, 4, 512], BF16) for i in range(2)]
            ogt = [sbt(st, f"B_og{i}", [128, 4, 512], BF16) for i in range(2)]
            lrt = [sbt(st, f"B_lr{i}", [17, 512], F32) for i in range(2)]
            for i in range(2):
                op("pool", lambda e, i=i: e.memset(lrt[i][:], 1.0), w=[f"B_lr{i}"])
            e_sb = sbt(st, "B_e", [128, 2, 256], F32)
            c_sb = sbt(st, "B_c", [128, 2, 256], F32)
            Eq = [sbt(st, f"B_Eq{i}", [128, 2, 256], F32) for i in range(2)]
            Ek = sbt(st, "B_Ek", [128, 2, 256], F32)
            ER = sbt(st, "B_ER", [128, 2, 256], F32)
            qd = [sbt(st, f"B_qd{i}", [128, 2, 256], BF16) for i in range(2)]
            ki = [sbt(st, f"B_ki{i}", [128, 2, 256], BF16) for i in range(2)]
            ke = [sbt(st, f"B_ke{i}", [128, 2, 256], BF16) for i in range(2)]
            attm = sbt(st, "B_attm", [128, 2, 2, 128], BF16)
            S32 = [sbt(st, f"B_S32{i}", [128, 2, 256], F32) for i in range(2)]
            Sbf = [sbt(st, f"B_Sbf{i}", [128, 2, 256], BF16) for i in range(4)]
            o_sb = sbt(st, "B_o", [128, 512], F32)
            sq = sbt(st, "B_sq", [128, 512], F32)
            ss = sbt(st, "B_ss", [128, 4], F32)
            rs = sbt(st, "B_rs", [128, 4], F32)
            ogn = [sbt(st, f"B_ogn{i}", [128, 4, 512], F32) for i in range(2)]
            gnb4 = sbt(st, "B_gnb4", [128, 512], F32)
            for h_ in range(4):
                op("pool", lambda e, h_=h_: e.tensor_copy(out=gnb4[:, h_ * 128:(h_ + 1) * 128], in_=gnb[:]), r=["B_cst"], w=["B_gnb4"])
            gl = sbt(st, "B_gl", [128, 512], BF16)
            gst = [sbt(st, f"B_gst{i}", [128, 4, 512], BF16) for i in range(2)]
            uT = sbt(st, "B_uT", [128, 2, 512], BF16)
            pbst = [sbt(st, f"B_pbst{i}", [128, 2, 512], BF16) for i in range(2)]
            PT = [sbt(st, f"B_PT{i}", [128, 5, 128], BF16) for i in range(2)]
            rcp = sbt(st, "B_rcp", [128, 4], F32)
            na_tm = sbt(st, "B_natm", [128, 256], BF16)
            nast = [sbt(st, f"B_nast{i}", [128, 2, 512], BF16) for i in range(2)]

            BK_Z, BK_ATT, BK_ATO, BK_O, BK_DS0, BK_DS1, BK_NA0, BK_NA1 = range(8)
            KZ, KATT, KATO, KNA0, KNA1 = "psZ", "psATT", "psATO", "psNA0", "psNA1"

            def gla_tile_loads(s, tl, dirn, par):
                t0 = tl * 512
                rows = slice(s * T + t0, s * T + t0 + 512)
                op("sp", lambda e: e.dma_start(out=qTt[par][:], in_=s_qT[s].rearrange("(p d) t -> d p t", d=128)[:, :, t0:t0 + 512]),
                   w=[f"B_qT{par}"], dsem=f"B_qT{par}")
                op("sp", lambda e: e.dma_start(out=kTt[par][:], in_=s_kT[s].rearrange("(p d) t -> d p t", d=128)[:, :, t0:t0 + 512]),
                   w=[f"B_kT{par}"], dsem=f"B_kT{par}")
                op("sp", lambda e: e.dma_start(out=kt[par][:], in_=s_k[rows, :].rearrange("(b p) c -> p b c", p=128)),
                   w=[f"B_k{par}"], dsem=f"B_k{par}")
                op("sp", lambda e: e.dma_start(out=vt[par][:], in_=s_v[rows, :].rearrange("(b p) c -> p b c", p=128)),
                   w=[f"B_v{par}"], dsem=f"B_v{par}")
                op("sp", lambda e: e.dma_start(out=lrt[par][0:16, :], in_=s_lrT[s, dirn * 16:(dirn + 1) * 16, t0:t0 + 512]),
                   w=[f"B_lr{par}"], dsem=f"B_lr{par}")
                if dirn == 1:
                    op("sp", lambda e: e.dma_start(out=ogt[par][:], in_=s_og[rows, :].rearrange("(b p) c -> p b c", p=128)),
                       w=[f"B_og{par}"], dsem=f"B_og{par}")
                    op("sp", lambda e: e.dma_start(out=oft[par][:], in_=s_of[rows, :].rearrange("(b p) c -> p b c", p=128)),
                       r=[f"dr_of_{s}_{tl}"], w=[f"B_of{par}"], dsem=f"B_of{par}")
                    for b_ in range(4):
                        op("pool", lambda e, b_=b_: e.tensor_tensor(out=ogn[par][:, b_, :], in0=ogt[par][:, b_, :], in1=gnb4[:], op=ALU.mult),
                           r=[f"B_og{par}", "B_gnb4"], w=[f"B_ogn{par}"])

            def gla_pass(s, dirn):
                NU = T // 256
                units = list(range(NU)) if dirn == 0 else list(range(NU - 1, -1, -1))
                for i_ in range(2):
                    op("pool", lambda e, i_=i_: e.memset(S32[i_][:], 0.0), w=[f"B_S32{i_}"])
                for c in range(2):
                    op("dve", lambda e, c=c: e.memset(PS[BK_DS0 + c][:], 0.0), w=[f"psDS{c}"])
                ring = {"i": 0}
                tiles_order = []
                for u in units:
                    if not tiles_order or tiles_order[-1] != u // 2:
                        tiles_order.append(u // 2)
                tpar = {tl: n % 2 for n, tl in enumerate(tiles_order)}
                gla_tile_loads(s, tiles_order[0], dirn, 0)

                def uparams(ui):
                    u = units[ui]
                    tl, uh = divmod(u, 2)
                    return u, tl, uh, tpar[tl], ui % 2, slice(uh * 256, uh * 256 + 256)

                def s1(ui):
                    u, tl, uh, par, q1, tu = uparams(ui)
                    zv = PS[BK_Z][:].rearrange("p (j c) -> p j c", j=2)
                    for j in range(2):
                        c0 = uh * 256 + j * 128
                        mm(zv[:, j, :], lrt[par][0:17, c0:c0 + 128], upw[0:17, dirn, :], True, True,
                           r=[f"B_lr{par}", "B_cst"], w=["psZ"], sig=(j == 1))
                    yield "s1"
                    op("act", lambda e: e.activation(out=e_sb[:], in_=zv, func=AF.Exp, scale=-1.0), r=["psZ"], w=["B_e"])
                    op("act", lambda e: e.activation(out=c_sb[:], in_=e_sb[:], func=AF.Ln, bias=1.0), r=["B_e"], w=["B_c"])
                    yield "s1"
                    btv = PS[BK_Z][:].rearrange("p (a t) -> p a t", a=2)
                    for j in range(2):
                        for p in range(2):
                            mm(btv[:, p, j * 128:(j + 1) * 128], c_sb[:, j, p * 128:(p + 1) * 128], tri[:, dirn * 2, :], True, True,
                               r=["B_c", "B_cst"], w=["psZ"], sig=(j == 1 and p == 1))
                    yield "s1"
                    op("act", lambda e: e.activation(out=Eq[q1][:], in_=btv, func=AF.Exp, scale=-1.0), r=["psZ"], w=[f"B_Eq{q1}"])
                    op("act", lambda e: e.activation(out=Ek[:], in_=btv, func=AF.Exp, scale=1.0), r=["psZ"], w=["B_Ek"])
                    yield "s1"
                    rv = PS[BK_Z][:].rearrange("p (j c) -> p j c", j=2)
                    for j in range(2):
                        mm(rv[:, j, :], tri[:, dirn * 2 + 1, :], c_sb[:, j, :], True, True, r=["B_c", "B_cst"], w=["psZ"], sig=(j == 1))
                    yield "s1"
                    op("act", lambda e: e.activation(out=ER[:], in_=rv, func=AF.Exp, scale=-1.0), r=["psZ"], w=["B_ER"])
                    yield "s1"
                    op("dve", lambda e: e.tensor_tensor(out=qd[q1][:], in0=qTt[par][:, :, tu], in1=Eq[q1][:], op=ALU.mult),
                       r=[f"B_qT{par}", f"B_Eq{q1}"], w=[f"B_qd{q1}"])
                    op("dve", lambda e: e.tensor_tensor(out=ki[q1][:], in0=kTt[par][:, :, tu], in1=Ek[:], op=ALU.mult),
                       r=[f"B_kT{par}", "B_Ek"], w=[f"B_ki{q1}"])
                    op("dve", lambda e: e.tensor_tensor(out=ke[q1][:], in0=kt[par][:, uh * 2:uh * 2 + 2, :], in1=ER[:], op=ALU.mult),
                       r=[f"B_k{par}", "B_ER"], w=[f"B_ke{q1}"])
                    yield "s1"

                def s2(ui):
                    u, tl, uh, par, q1, tu = uparams(ui)
                    n = tiles_order.index(tl)
                    first_of_tile = (ui == 0) or (units[ui - 1] // 2 != tl)
                    if first_of_tile and n + 1 < len(tiles_order):
                        gla_tile_loads(s, tiles_order[n + 1], dirn, 1 - par)
                    jorder = [0, 1] if dirn == 0 else [1, 0]
                    corder = [0, 1] if dirn == 0 else [1, 0]
                    for j in jorder:
                        b = uh * 2 + j
                        tb = slice(b * 128, (b + 1) * 128)
                        tj = slice(j * 128, (j + 1) * 128)
                        for c in range(2):
                            dsv = PS[BK_DS0 + c][:].rearrange("p (a t) -> p a t", a=2)
                            for h in range(4):
                                pb = (h % 2) * 64
                                cs = slice((h % 2) * 128, (h % 2) * 128 + 128)
                                mm(dsv[pb:pb + 64, h // 2, cs], ke[q1][c * 64:(c + 1) * 64, j, h * 64:(h + 1) * 64],
                                   vt[par][c * 64:(c + 1) * 64, b, h * 128:(h + 1) * 128], True, True,
                                   r=[f"B_ke{q1}", f"B_v{par}"], w=[f"psDS{c}"], sig=(h == 3))
                        snaps = {}
                        for c in corder:
                            k = ring["i"] % 4
                            cur = ring["i"] % 2
                            ring["i"] += 1
                            snaps[c] = k
                            op("act", lambda e, k=k, cur=cur: e.activation(out=Sbf[k][:], in_=S32[cur][:], func=AF.Copy),
                               r=[f"B_S32{cur}"], w=[f"B_Sbf{k}"])
                            col = j * 128 + c * 64 + (63 if dirn == 0 else 0)
                            dsv = PS[BK_DS0 + c][:].rearrange("p (a t) -> p a t", a=2)
                            for p in range(2):
                                op("dve", lambda e, p=p, col=col, cur=cur, dsv=dsv: e.scalar_tensor_tensor(
                                    out=S32[1 - cur][:, p, :], in0=S32[cur][:, p, :], scalar=Eq[q1][:, p, col:col + 1],
                                    in1=dsv[:, p, :], op0=ALU.mult, op1=ALU.add),
                                   r=[f"B_S32{cur}", f"B_Eq{q1}", f"psDS{c}"], w=[f"B_S32{1 - cur}"])
                        att_e = PS[BK_ATT][:, 0:256].rearrange("p (a t) -> p a t", a=2)
                        att_o = PS[BK_ATO][:, 0:256].rearrange("p (a t) -> p a t", a=2)
                        for h in (0, 2):
                            mm(att_e[:, h // 2, :], ki[q1][0:64, h // 2, tj], qd[q1][0:64, h // 2, tj], True, True,
                               r=[f"B_ki{q1}", f"B_qd{q1}"], w=["psATT"], sig=(h == 2))
                        for h in (1, 3):
                            mm(att_o[:, h // 2, :], ki[q1][64:128, h // 2, tj], qd[q1][64:128, h // 2, tj], True, True,
                               r=[f"B_ki{q1}", f"B_qd{q1}"], w=["psATO"], sig=(h == 3))
                        op("dve", lambda e: e.tensor_tensor(out=attm[:, 0], in0=att_e, in1=msk[:, dirn, 0:2, :], op=ALU.mult),
                           r=["psATT", "B_cst"], w=["B_attm"])
                        op("dve", lambda e: e.tensor_tensor(out=attm[:, 1], in0=att_o, in1=msk[:, dirn, 0:2, :], op=ALU.mult),
                           r=["psATO", "B_cst"], w=["B_attm"])
                        yield "sub"
                        for h in range(4):
                            mm(PS[BK_O][:, h * 128:(h + 1) * 128], attm[:, h % 2, h // 2, :], vt[par][:, b, h * 128:(h + 1) * 128],
                               h == 0, False, r=["B_attm", f"B_v{par}"], w=["psO"], sig=False, skip=True)
                        for ci, c in enumerate(corder):
                            k = snaps[c]
                            for p in range(2):
                                mm(PS[BK_O][c * 64:(c + 1) * 64, p * 256:(p + 1) * 256], qd[q1][:, p, j * 128 + c * 64: j * 128 + (c + 1) * 64],
                                   Sbf[k][:, p, :], False, (ci == 1 and p == 1), r=[f"B_qd{q1}", f"B_Sbf{k}"], w=["psO"],
                                   sig=(p == 1), skip=True)
                        yield "sub"
                        if dirn == 0:
                            evac(oft[par][:, b, :], PS[BK_O][:], r=["psO"], w=[f"B_of{par}"], eng="act")
                        else:
                            op("dve", lambda e: e.tensor_tensor(out=o_sb[:], in0=PS[BK_O][:], in1=oft[par][:, b, :], op=ALU.add),
                               r=["psO", f"B_of{par}"], w=["B_o"])
                            op("act", lambda e: e.activation(out=sq[:], in_=o_sb[:], func=AF.Square), r=["B_o"], w=["B_sq"])
                            op("dve", lambda e: e.tensor_reduce(out=ss[:], in_=sq[:].rearrange("p (h d) -> p h d", h=4),
                                                                axis=AX.X, op=ALU.add), r=["B_sq"], w=["B_ss"])
                            op("dve", lambda e: e.tensor_scalar(out=ss[:], in0=ss[:], scalar1=1.0 / 128.0, scalar2=RMS_EPS,
                                                                op0=ALU.mult, op1=ALU.add), r=["B_ss"], w=["B_ss"])
                            op("act", lambda e: e.activation(out=rs[:], in_=ss[:], func=AF.Ln), r=["B_ss"], w=["B_rs"])
                            op("act", lambda e: e.activation(out=rs[:], in_=rs[:], func=AF.Exp, scale=-0.5), r=["B_rs"], w=["B_rs"])
                            for h in range(4):
                                op("dve", lambda e, h=h: e.scalar_tensor_tensor(
                                    out=gl[:, h * 128:(h + 1) * 128], in0=o_sb[:, h * 128:(h + 1) * 128], scalar=rs[:, h:h + 1],
                                    in1=ogn[par][:, b, h * 128:(h + 1) * 128], op0=ALU.mult, op1=ALU.mult),
                                   r=["B_o", "B_rs", f"B_ogn{par}"], w=["B_gl"])
                            trv = PS[BK_ATT][:].bitcast(BF16)[:, 512:1024].rearrange("p (k t) -> p k t", k=4)
                            for k4 in range(4):
                                op("pe", lambda e, k4=k4: e.transpose(trv[:, k4, :], gl[:, k4 * 128:(k4 + 1) * 128], ident[:]),
                                   r=["B_gl", "ident"], w=["psATT"], sig=(k4 == 3))
                            evac(gst[par][:, :, tb], trv, r=["psATT"], w=[f"B_gst{par}"])
                        yield "blk"
                    last_of_tile = (ui == len(units) - 1) or (units[ui + 1] // 2 != tl)
                    if last_of_tile:
                        if dirn == 0:
                            rws = slice(s * T + tl * 512, s * T + tl * 512 + 512)
                            op("sp", lambda e: e.dma_start(out=s_of[rws, :].rearrange("(b p) c -> p b c", p=128), in_=oft[par][:]),
                               r=[f"B_of{par}"], w=[f"dr_of_{s}_{tl}"], dsem=f"B_ofs{par}")
                        else:
                            t0 = tl * 512
                            op("sp", lambda e: e.dma_start(out=s_gT[s].rearrange("(c p) t -> p c t", p=128)[:, :, t0:t0 + 512],
                                                           in_=gst[par][:]), r=[f"B_gst{par}"], dsem=f"B_gst{par}")


                for _ in s1(0):
                    pass
                for ui in range(len(units)):
                    g1 = s1(ui + 1) if ui + 1 < len(units) else iter(())
                    g2 = s2(ui)
                    a1 = a2 = True
                    while a1 or a2:
                        if a2:
                            ev = next(g2, None)
                            if ev is None:
                                a2 = False
                            else:
                                yield ev
                        if a1:
                            if next(g1, None) is None:
                                a1 = False

            def pool_pass(s):
                for i in range(NB):
                    yield i

            def seq_loads(s):
                rows = slice(s * T, (s + 1) * T)
                op("sp", lambda e: e.dma_start(out=pT, in_=s_p[rows, :].rearrange("(b p) c -> p b c", p=128)), w=["B_nq"], dsem="B_nq")
                op("sp", lambda e: e.dma_start(out=nk[:], in_=s_nkT[s].rearrange("(p d) t -> d p t", d=128)), w=["B_nk"], dsem="B_nk")
                op("sp", lambda e: e.dma_start(out=nvx[:], in_=s_nv[rows, :].rearrange("(b p) c -> p b c", p=128)), w=["B_nvx"], dsem="B_nvx")

            def seq_loads_bwd(s):
                op("sp", lambda e: e.dma_start(out=nq[:], in_=s_nqT[s].rearrange("(p d) t -> d p t", d=128)), w=["B_nq"], dsem="B_nq")

            def pool_block(s, i):
                bku = BK_NA0
                uv = PS[bku][:, 0:256].rearrange("p (a t) -> p a t", a=2)
                ds_ = [d for d in (-1, 0, 1) if 0 <= i + d < NB]
                for g in range(4):
                    pb = (g % 2) * 64
                    for n, d in enumerate(ds_):
                        var = d + 1
                        if d == 0 and i == 0:
                            var = 3
                        if d == 0 and i == NB - 1:
                            var = 4
                        mm(uv[pb:pb + 64, g // 2, :], pT[:, i + d, g * 64:(g + 1) * 64], bmat[:, var, g, :],
                           n == 0, n == len(ds_) - 1, r=["B_nq", "B_bmat"], w=["psNA0"], sig=(g == 3 and n == len(ds_) - 1))
                j = i % 4
                evac(uT[:, :, j * 128:(j + 1) * 128], uv, r=["psNA0"], w=["B_uT"], eng="act")
                if j == 3:
                    par = (i // 4) % 2
                    for p in range(2):
                        mm(PS[BK_NA0][:, :], pwbf[:, p, :], uT[:, p, :], True, True, r=["B_pwbf", "B_uT"], w=["psNA0"])
                        op("act", lambda e, p=p: e.activation(out=pbst[par][:, p, :], in_=PS[BK_NA0][:, :], func=AF.Copy,
                                                              scale=psc[:, p:p + 1]), r=["psNA0", "B_cst"], w=[f"B_pbst{par}"])
                    t0 = (i // 4) * 512
                    op("sp", lambda e: e.dma_start(out=s_pbT[s].rearrange("(c p) t -> p c t", p=128)[:, :, t0:t0 + 512],
                                                   in_=pbst[par][:]), r=[f"B_pbst{par}"], dsem=f"B_pbst{par}")

            def na_gen(s):
                unitsn = [(i, h) for i in range(NB) for h in range(4)]
                onv = PS[BK_NA1][:, 0:260].rearrange("p (h d) -> p h d", h=4)
                s4v = PS[BK_NA1][:, 384:512]

                def scores(i, h):
                    chs = chunks_of(i)
                    pb = (h % 2) * 64
                    qs = slice(i * 128, (i + 1) * 128)
                    for n, j in enumerate(chs):
                        if n < 4:
                            dst, key = PS[BK_NA0][:, n * 128:(n + 1) * 128], KNA0
                        else:
                            dst, key = s4v, KNA1
                        mm(dst, nk[pb:pb + 64, h // 2, j * 128:(j + 1) * 128], nq[pb:pb + 64, h // 2, qs], True, True,
                           r=["B_nk", "B_nq"], w=[key], sig=(n == len(chs) - 1 or n == 3))

                def probs(i, h, k):
                    chs = chunks_of(i)
                    nch = len(chs)
                    tb0 = tile_base(i)
                    pp = k % 2
                    op("act", lambda e: e.activation(out=PT[pp][:, 0:4, :], in_=PS[BK_NA0][:].rearrange("p (a t) -> p a t", a=4),
                                                     func=AF.Exp), r=[KNA0], w=[f"B_PT{pp}", f"B_PTa{pp}", f"B_PTb{pp}"])
                    if nch == 5:
                        op("act", lambda e: e.activation(out=PT[pp][:, 4, :], in_=s4v, func=AF.Exp), r=[KNA1],
                           w=[f"B_PT{pp}", f"B_PTa{pp}", f"B_PTb{pp}"])
                    op("dve", lambda e: e.tensor_tensor(out=PT[pp][:, 0:3, :], in0=PT[pp][:, 0:3, :],
                                                        in1=nab[:, tb0:tb0 + 3, h, :], op=ALU.mult),
                       r=[f"B_PT{pp}", "B_nab"], w=[f"B_PTa{pp}"])
                    op("pool", lambda e: e.tensor_tensor(out=PT[pp][:, 3:nch, :], in0=PT[pp][:, 3:nch, :],
                                                         in1=nab[:, tb0 + 3:tb0 + nch, h, :], op=ALU.mult),
                       r=[f"B_PT{pp}", "B_nab"], w=[f"B_PTb{pp}"])

                def pv(i, h, k):
                    chs = chunks_of(i)
                    pp = k % 2
                    for n, j in enumerate(chs):
                        mm(onv[:, h, 0:65], PT[pp][:, n, :], nvx[:, j, h * 65:(h + 1) * 65], n == 0, n == len(chs) - 1,
                           r=[f"B_PT{pp}", f"B_PTa{pp}", f"B_PTb{pp}", "B_nvx"], w=[KNA1], sig=(n == len(chs) - 1))

                def finish(i):
                    op("dve", lambda e: e.reciprocal(out=rcp[:].rearrange("p (h o) -> p h o", o=1), in_=onv[:, :, 64:65]), r=[KNA1], w=["B_rcp"])
                    for h in range(4):
                        op("dve", lambda e, h=h: e.tensor_scalar(out=na_tm[:, h * 64:(h + 1) * 64], in0=onv[:, h, 0:64],
                                                                 scalar1=rcp[:, h:h + 1], scalar2=None, op0=ALU.mult),
                           r=[KNA1, "B_rcp"], w=["B_natm"])
                    trv = PS[BK_ATO][:].bitcast(BF16)[:, 768:1024].rearrange("p (k t) -> p k t", k=2)
                    for k in range(2):
                        op("pe", lambda e, k=k: e.transpose(trv[:, k, :], na_tm[:, k * 128:(k + 1) * 128], ident[:]),
                           r=["B_natm", "ident"], w=[KATO], sig=(k == 1))
                    j = i % 4
                    par = (i // 4) % 2
                    evac(nast[par][:, :, j * 128:(j + 1) * 128], trv, r=[KATO], w=[f"B_nast{par}"], eng="act")
                    if j == 3:
                        t0 = (i // 4) * 512
                        op("sp", lambda e: e.dma_start(out=s_naT[s].rearrange("(c p) t -> p c t", p=128)[:, :, t0:t0 + 512],
                                                       in_=nast[par][:]), r=[f"B_nast{par}"], dsem=f"B_nast{par}")

                scores(*unitsn[0])
                for k, (i, h) in enumerate(unitsn):
                    probs(i, h, k)
                    if k + 1 < len(unitsn):
                        scores(*unitsn[k + 1])
                    pv(i, h, k)
                    if h == 3:
                        finish(i)
                    yield

            def load_nab():
                for t in range(21):
                    op("pool", lambda e, t=t: e.dma_start(out=nab[:, t], in_=nabias[l][:, t]), w=["B_nab"], dsem="B_nab")

            for s in range(NS):
                g0 = gla_pass(s, 0)
                ev = next(g0)
                seq_loads(s)
                if s == 0:
                    load_nab()
                i = 0
                while ev is not None:
                    if ev == "blk":
                        pool_block(s, i)
                        i += 1
                    ev = next(g0, None)
                seq_loads_bwd(s)
                if s == 0:
                    for t in range(0, 21, 3):
                        op("act", lambda e, t=t: e.activation(out=nab[:, t:t + 3], in_=nab[:, t:t + 3], func=AF.Exp), r=["B_nab"], w=["B_nab"])
                nag = na_gen(s)
                for ev in gla_pass(s, 1):
                    next(nag, None)
                for _ in nag:
                    pass
            S.barrier()

    def phase_C(l, xsrc):
        with ExitStack() as st:
            Wg = sbt(st, "C_wg", [128, 8, 3072], BF16)
            Wa = sbt(st, "C_wa", [128, 4, D], BF16)
            Wb = sbt(st, "C_wb", [128, 2, D], BF16)
            Wc = sbt(st, "C_wc", [128, 2, D], BF16)
            Wo = sbt(st, "C_wo", [128, 8, D], BF16)
            load_w(Wg, w_in[l][:, NMIX:DIN], 8, "C_w", "C_w")
            load_w(Wa, w_bra[l], 4, "C_w", "C_w")
            load_w(Wb, w_brb[l], 2, "C_w", "C_w")
            load_w(Wc, w_brc[l], 2, "C_w", "C_w")
            load_w(Wo, w_out[l], 8, "C_wo", "C_wo")
            lg = sbt(st, "C_lg", [128, D], F32)
            lb = sbt(st, "C_lb", [128, D], F32)
            op("sp", lambda e: e.dma_start(out=lg[:], in_=lnp[l, 0].partition_broadcast(128)), w=["lnp"], dsem="lnp")
            op("sp", lambda e: e.dma_start(out=lb[:], in_=lnp[l, 1].partition_broadcast(128)), w=["lnp"], dsem="lnp")
            xblk = [sbt(st, f"C_x{i}", [128, D], F32) for i in range(8)]
            xbf = [sbt(st, f"C_xbf{i}", [128, D], BF16) for i in range(2)]
            xT = [sbt(st, f"C_xT{i}", [128, 8, 512], BF16) for i in range(2)]
            gT = [sbt(st, f"C_gT{i}", [128, 4, 512], BF16) for i in range(2)]
            pbT = [sbt(st, f"C_pbT{i}", [128, 2, 512], BF16) for i in range(2)]
            naT = [sbt(st, f"C_naT{i}", [128, 2, 512], BF16) for i in range(2)]
            sg = [sbt(st, f"C_sg{i}", [128, 3, 512], F32) for i in range(2)]
            t1 = [sbt(st, f"C_t{i}", [128, 3, 512], F32) for i in range(2)]
            mT = sbt(st, "C_mT", [128, 8, 512], BF16)
            stats = sbt(st, "C_stats", [128, 4, 12], F32)
            mv = sbt(st, "C_mv", [128, 4, 4], F32)
            rstd = sbt(st, "C_rstd", [128, 4, 1], F32)

            def loads(ti):
                par = ti % 2
                s, t0 = divmod(ti, NTL)
                t0 *= 512
                for b in range(4):
                    i = par * 4 + b
                    op("sp", lambda e, i=i, b=b: e.dma_start(out=xblk[i][:], in_=xsrc[ti * 512 + b * 128: ti * 512 + (b + 1) * 128, :]),
                       w=[f"C_x{i}"], dsem=f"C_x{i}")
                for (dst, scr, key) in ((gT, s_gT, "C_gT"), (pbT, s_pbT, "C_pbT"), (naT, s_naT, "C_naT")):
                    op("sp", lambda e, dst=dst, scr=scr: e.dma_start(
                        out=dst[par][:], in_=scr[s].rearrange("(c p) t -> p c t", p=128)[:, :, t0:t0 + 512]),
                       w=[f"{key}{par}"], dsem=f"{key}{par}")

            def mk_xT(ti):
                par = ti % 2
                for b in range(4):
                    make_xT(xblk[par * 4 + b][:], f"C_x{par * 4 + b}", xbf[b % 2], f"C_xbf{b % 2}", xT[par], f"C_xT{par}", b,
                            nextbank(2, 6), ceng="act")

            dfr = Deferred()
            loads(0)
            mk_xT(0)
            if NTILE > 1:
                loads(1)
            for ti in range(NTILE):
                par = ti % 2
                xk = f"C_xT{par}"
                for ch in range(8):
                    q = ch % 2
                    cs = slice(ch * 128, (ch + 1) * 128)
                    for gi in range(3):
                        bk = gi
                        for k in range(8):
                            mm(PS[bk][:], Wg[:, k, gi * D + ch * 128: gi * D + (ch + 1) * 128], xT[par][:, k, :], k == 0, k == 7,
                               r=["C_w", xk], w=[f"ps{bk}"])
                        op("act", lambda e, gi=gi, bk=bk: e.activation(out=sg[q][:, gi, :], in_=PS[bk][:], func=AF.Sigmoid),
                           r=[f"ps{bk}"], w=[f"C_sg{q}"])
                    for gi, (Wx, src, nk_, key) in enumerate(((Wa, gT, 4, "C_gT"), (Wb, pbT, 2, "C_pbT"), (Wc, naT, 2, "C_naT"))):
                        bk = 3 + gi
                        for k in range(nk_):
                            mm(PS[bk][:], Wx[:, k, cs], src[par][:, k, :], k == 0, k == nk_ - 1,
                               r=["C_w", f"{key}{par}"], w=[f"ps{bk}"])
                        op("dve", lambda e, gi=gi, bk=bk: e.tensor_tensor(out=t1[q][:, gi, :], in0=PS[bk][:], in1=sg[q][:, gi, :],
                                                                          op=ALU.mult), r=[f"ps{bk}", f"C_sg{q}"], w=[f"C_t{q}"])
                    op("pool", lambda e: e.tensor_tensor(out=t1[q][:, 0, :], in0=t1[q][:, 0, :], in1=t1[q][:, 1, :], op=ALU.add),
                       r=[f"C_t{q}"], w=[f"C_t{q}"])
                    op("pool", lambda e, ch=ch: e.tensor_tensor(out=mT[:, ch, :], in0=t1[q][:, 0, :], in1=t1[q][:, 2, :], op=ALU.add),
                       r=[f"C_t{q}"], w=["C_mT"])
                    dfr.advance(5)
                dfr.drain()
                if ti + 1 < NTILE:
                    mk_xT(ti + 1)
                for b in range(4):
                    xi = par * 4 + b
                    xkey = f"C_x{xi}"
                    for n in range(2):
                        bk = nextbank(6, 0)
                        for k in range(8):
                            mm(PS[bk][:], mT[:, k, b * 128:(b + 1) * 128], Wo[:, k, n * 512:(n + 1) * 512], k == 0, k == 7,
                               r=["C_wo", "C_mT"], w=[f"ps{bk}"])
                        op("dve", lambda e, n=n, bk=bk, xi=xi: e.scalar_tensor_tensor(
                            out=xblk[xi][:, n * 512:(n + 1) * 512], in0=xblk[xi][:, n * 512:(n + 1) * 512], scalar=ALPHA,
                            in1=PS[bk][:], op0=ALU.mult, op1=ALU.add), r=[xkey, f"ps{bk}"], w=[xkey])

                def ln_store(b, par=par, ti=ti):
                    xi = par * 4 + b
                    xkey = f"C_x{xi}"
                    yield from layer_norm(st, xblk[xi][:], xkey, lg, lb, f"C_{b}", stats[:, b], mv[:, b], rstd[:, b])
                    op("sp", lambda e: e.dma_start(out=X1[ti * 512 + b * 128: ti * 512 + (b + 1) * 128, :], in_=xblk[xi][:]),
                       r=[xkey], dsem=f"C_xs{xi}")
                nxt_loads = (lambda t=ti + 2: loads(t)) if ti + 2 < NTILE else None
                dfr.set([ln_store(b) for b in range(4)], after=nxt_loads)
            dfr.drain()
            S.barrier()

    def phase_D(l, xdst):
        NJ = DFF // 128
        with ExitStack() as st:
            Wg = sbt(st, "D_wg", [128, 8, DFF], BF16)
            Wu = sbt(st, "D_wu", [128, 8, DFF], BF16)
            Wd = sbt(st, "D_wd", [128, NJ, D], BF16)
            GJ = [(0, 6), (6, 14), (14, NJ)]
            jgrp = {}
            for g, (j0, j1) in enumerate(GJ):
                for j in range(j0, j1):
                    jgrp[j] = g
                for (Wx, wsrc, nm) in ((Wg, w_gate, "g"), (Wu, w_up, "u")):
                    for k in range(8):
                        op("pool", lambda e, Wx=Wx, wsrc=wsrc, k=k, j0=j0, j1=j1: e.dma_start(
                            out=Wx[:, k, j0 * 128:j1 * 128], in_=wsrc[l][k * 128:(k + 1) * 128, j0 * 128:j1 * 128]),
                           w=[f"D_wgu{g}"], dsem=f"D_wgu{g}")
            for g, (j0, j1) in enumerate(GJ):
                for j in range(j0, j1):
                    op("pool", lambda e, j=j: e.dma_start(out=Wd[:, j, :], in_=w_down[l][j * 128:(j + 1) * 128, :]),
                       w=["D_wd"], dsem="D_wd")
            lg = sbt(st, "D_lg", [128, D], F32)
            lb = sbt(st, "D_lb", [128, D], F32)
            op("sp", lambda e: e.dma_start(out=lg[:], in_=lnp[l, 2].partition_broadcast(128)), w=["lnp"], dsem="lnp")
            op("sp", lambda e: e.dma_start(out=lb[:], in_=lnp[l, 3].partition_broadcast(128)), w=["lnp"], dsem="lnp")
            xs = [sbt(st, f"D_xs{i}", [128, D], F32) for i in range(2)]
            xr = [sbt(st, f"D_xr{i}", [128, D], F32) for i in range(4)]
            xbf = sbt(st, "D_xbf", [128, D], BF16)
            xT = sbt(st, "D_xT", [128, 8, 512], BF16)
            hT = sbt(st, "D_hT", [128, NJ, 512], BF16)
            sgs = [sbt(st, f"D_sg{i}", [128, 512], F32) for i in range(2)]
            stats = sbt(st, "D_stats", [128, 4, 12], F32)
            mv = sbt(st, "D_mv", [128, 4, 4], F32)
            rstd = sbt(st, "D_rstd", [128, 4, 1], F32)

            def rows(ti, b_):
                return slice(ti * 512 + b_ * 128, ti * 512 + (b_ + 1) * 128)

            def load_xs(ti, b_):
                i = b_ % 2
                op("sp", lambda e: e.dma_start(out=xs[i][:], in_=X1[rows(ti, b_), :]), w=[f"D_xs{i}"], dsem=f"D_xs{i}")

            def load_xr(ti):
                for b_ in range(4):
                    op("sp", lambda e, b_=b_: e.dma_start(out=xr[b_][:], in_=X1[rows(ti, b_), :]), w=[f"D_xr{b_}"], dsem=f"D_xr{b_}")

            def mk_xT(ti):
                for b_ in range(4):
                    make_xT(xs[b_ % 2][:], f"D_xs{b_ % 2}", xbf, "D_xbf", xT, "D_xT", b_, nextbank(2, 6), ceng="act")
                    if b_ + 2 < 4:
                        load_xs(ti, b_ + 2)

            dfr = Deferred()
            load_xs(0, 0)
            load_xs(0, 1)
            mk_xT(0)
            load_xr(0)
            for ti in range(NTILE):
                if ti + 1 < NTILE:
                    load_xs(ti + 1, 0)
                    load_xs(ti + 1, 1)
                for j in range(NJ):
                    q = j % 2
                    cs = slice(j * 128, (j + 1) * 128)
                    bg = 0 + q
                    bu = 2 + q
                    for k in range(8):
                        mm(PS[bg][:], Wg[:, k, cs], xT[:, k, :], k == 0, k == 7, r=[f"D_wgu{jgrp[j]}", "D_xT"], w=[f"ps{bg}"])
                    for k in range(8):
                        mm(PS[bu][:], Wu[:, k, cs], xT[:, k, :], k == 0, k == 7, r=[f"D_wgu{jgrp[j]}", "D_xT"], w=[f"ps{bu}"])
                    op("act", lambda e, q=q, bg=bg: e.activation(out=sgs[q][:], in_=PS[bg][:], func=AF.Silu), r=[f"ps{bg}"], w=[f"D_sg{q}"])
                    op("dve", lambda e, q=q, bu=bu, j=j: e.tensor_tensor(out=hT[:, j, :], in0=PS[bu][:], in1=sgs[q][:], op=ALU.mult),
                       r=[f"ps{bu}", f"D_sg{q}"], w=["D_hT"])
                    dfr.advance(2)
                dfr.drain()
                if ti + 1 < NTILE:
                    mk_xT(ti + 1)
                for b_ in range(4):
                    xkey = f"D_xr{b_}"
                    for n in range(2):
                        bk = nextbank(6, 0)
                        for j in range(NJ):
                            mm(PS[bk][:], hT[:, j, b_ * 128:(b_ + 1) * 128], Wd[:, j, n * 512:(n + 1) * 512], j == 0, j == NJ - 1,
                               r=["D_wd", "D_hT"], w=[f"ps{bk}"])
                        op("dve", lambda e, n=n, bk=bk, b_=b_: e.scalar_tensor_tensor(
                            out=xr[b_][:, n * 512:(n + 1) * 512], in0=xr[b_][:, n * 512:(n + 1) * 512], scalar=ALPHA,
                            in1=PS[bk][:], op0=ALU.mult, op1=ALU.add), r=[xkey, f"ps{bk}"], w=[xkey])

                def ln_store(b_, ti=ti):
                    xkey = f"D_xr{b_}"
                    yield from layer_norm(st, xr[b_][:], xkey, lg, lb, f"D_{b_}", stats[:, b_], mv[:, b_], rstd[:, b_])
                    op("sp", lambda e: e.dma_start(out=xdst[rows(ti, b_), :], in_=xr[b_][:]), r=[xkey], dsem=f"D_xo{b_}")
                nxt = (lambda t=ti + 1: load_xr(t)) if ti + 1 < NTILE else None
                dfr.set([ln_store(b_) for b_ in range(4)], after=nxt)
            dfr.drain()
            S.barrier()

    S.barrier()
    import os as _os
    stop = _os.environ.get("K_STOP", "")
    for l in range(L):
        xsrc = x_in if l == 0 else X2
        if stop == "0":
            break
        phase_A(l, xsrc)
        if stop == "A":
            break
        phase_B(l)
        if stop == "B":
            break
        phase_C(l, xsrc)
        if stop == "C":
            break
        phase_D(l, y_out if l == L - 1 else X2)
    glob.close()
    return nc, S


def host_inputs(T, L, w):
    NB = T // 128
    c = _consts(T)
    ri, ci, va = _na_bias_index(NB)
    rpb = np.asarray(w["na_rpb"], np.float32)
    nab = rpb[:, :, ri, ci]
    nab = np.where(va[None, None], nab, np.float32(NEG))
    nab = np.ascontiguousarray(nab.transpose(0, 3, 2, 1, 4))
    upx = np.zeros((L, 2, 17, 256), np.float32)
    upx[:, 0, :16] = w["gla_up_f"]
    upx[:, 1, :16] = w["gla_up_b"]
    upx[:, 0, 16] = w["gla_bias_f"]
    upx[:, 1, 16] = w["gla_bias_b"]
    lnp = np.stack([w["ln1_g"], w["ln1_b"], w["ln2_g"], w["ln2_b"]], axis=1).astype(np.float32)
    m = {
        "w_in": w["w_in"], "w_br_a": w["w_br_a"], "w_br_b": w["w_br_b"], "w_br_c": w["w_br_c"], "w_out": w["w_out"],
        "w_gate": w["w_gate"], "w_up": w["w_up"], "w_down": w["w_down"], "upx": upx, "gla_norm": w["gla_norm"],
        "pool_w": w["pool_w"], "pool_scale": np.asarray(w["pool_scale"]).reshape(L, 2, 128).transpose(0, 2, 1), "nabias": nab, "lnp": lnp,
        "ident": c["ident"], "tri": np.stack([c["trix_f"], c["tris_f"], c["trix_b"], c["tris_b"]]),
        "mask": np.stack([c["mask_f"], c["mask_b"]]), "bmat": c["bmat"], "negh": c["negh"],
    }
    return {k: np.ascontiguousarray(np.asarray(v, np.float32)) for k, v in m.items()}


def kernel(**inputs):
    L, T, NS, NCORE = 4, 4096, 3, 8
    xp = np.asarray(inputs["x_prompt"], np.float32)
    xs = np.asarray(inputs["x_sample"], np.float32)
    seqs = [xp[i] for i in range(xp.shape[0])] + [xs[i] for i in range(xs.shape[0])]
    nseq = len(seqs)
    assign = [[(c + 8 * j) % nseq if (c + 8 * j) < nseq else c for j in range(NS)] for c in range(NCORE)]
    shared = host_inputs(T, L, inputs)
    nc, _ = build(NS, T, L)
    in_maps = []
    for c in range(NCORE):
        m = dict(shared)
        m["x"] = np.ascontiguousarray(np.concatenate([seqs[i] for i in assign[c]], axis=0))
        in_maps.append(m)
    res = run_bass_kernel_spmd(nc, in_maps, core_ids=list(range(NCORE)))
    outs = [None] * nseq
    for c in range(NCORE):
        y = np.asarray(res.results[c]["y"], np.float32).reshape(NS, T, D)
        for j in range(NS):
            if c + 8 * j < nseq:
                outs[c + 8 * j] = y[j]
    y_prompt = np.stack(outs[:xp.shape[0]], axis=0)
    y_sample = np.stack(outs[xp.shape[0]:], axis=0)
    return (y_prompt, y_sample)
```

```python
import numpy as np
from contextlib import ExitStack
import concourse.bass as bass
import concourse.mybir as mybir
from concourse.bass_utils import run_bass_kernel_spmd

F32 = mybir.dt.float32
BF16 = mybir.dt.bfloat16
AF = mybir.ActivationFunctionType
ALU = mybir.AluOpType
AX = mybir.AxisListType

D = 1024
DFF = 2816
DIN = 5664
NMIX = 2592
ALPHA = 8.0 ** 0.25
LN_EPS = 1e-5
RMS_EPS = 1e-6
EPOCH = 12000
NEG = -30000.0


class Tok:
    __slots__ = ("eng", "sig", "dma")

    def __init__(self, eng, dma=False):
        self.eng = eng
        self.sig = None
        self.dma = dma


class Sched:
    def __init__(self, nc):
        self.nc = nc
        self.engs = {"pe": nc.tensor, "act": nc.scalar, "dve": nc.vector, "pool": nc.gpsimd, "sp": nc.sync}
        self.sems = {}
        self.cnt = {e: 0 for e in ("pe", "act", "dve", "pool")}
        self.dcnt = {}
        self.lastw = {}
        self.readers = {}
        self.waited = {e: {} for e in self.engs}
        self.pe_pending = []
        self.last_tok = {}
        self.barc = 0
        self.nins = 0

    def sem(self, name):
        if name not in self.sems:
            self.sems[name] = self.nc.alloc_semaphore(name)
        return self.sems[name]

    def _wait(self, eng, tok):
        if tok.sig is None:
            raise RuntimeError("dependency on unsignalled PE op")
        name, val = tok.sig
        if self.waited[eng].get(name, 0) >= val:
            return
        self.waited[eng][name] = val
        self.engs[eng].wait_ge(self.sem(name), val)
        self.nins += 1

    def op(self, eng, fn, r=(), w=(), sig=True, dsem=None):
        deps = []
        for k in r:
            t = self.lastw.get(k)
            if t is not None:
                deps.append(t)
        for k in w:
            t = self.lastw.get(k)
            if t is not None:
                deps.append(t)
            rd = self.readers.get(k)
            if rd:
                for e, v in rd.items():
                    if e == "dma":
                        deps.extend(v)
                    else:
                        deps.append(v)
        is_dma = dsem is not None
        best = {}
        for t in deps:
            if t.eng == "pe" and eng == "pe" and not is_dma and not t.dma:
                continue
            if t.sig is None:
                raise RuntimeError("dependency on unsignalled PE op")
            nm, v = t.sig
            if nm not in best or best[nm].sig[1] < v:
                best[nm] = t
        for t in best.values():
            self._wait(eng, t)
        ins = fn(self.engs[eng])
        self.nins += 1
        tok = Tok(eng, dma=is_dma)
        if is_dma:
            c = self.dcnt.get(dsem, 0) + 16
            self.dcnt[dsem] = c
            ins.then_inc(self.sem(dsem), 16)
            tok.sig = (dsem, c)
        elif eng == "pe" and not sig:
            self.pe_pending.append(tok)
        else:
            c = self.cnt[eng]
            ep, v = divmod(c, EPOCH)
            self.cnt[eng] = c + 1
            name = f"{eng}{ep}"
            ins.then_inc(self.sem(name), 1)
            tok.sig = (name, v + 1)
            self.last_tok[eng] = tok
            if eng == "pe":
                for p in self.pe_pending:
                    p.sig = tok.sig
                self.pe_pending = []
        for k in r:
            d = self.readers.setdefault(k, {})
            if is_dma:
                d.setdefault("dma", []).append(tok)
            else:
                d[eng] = tok
        for k in w:
            self.lastw[k] = tok
            self.readers[k] = {}
        return tok

    def barrier(self):
        assert not self.pe_pending
        sp = "sp"
        for e, t in self.last_tok.items():
            self._wait(sp, t)
        for name, c in self.dcnt.items():
            if self.waited[sp].get(name, 0) < c:
                self.waited[sp][name] = c
                self.engs[sp].wait_ge(self.sem(name), c)
        self.barc += 1
        self.engs[sp].nop().then_inc(self.sem("bar"), 1)
        for e in ("pe", "act", "dve", "pool"):
            self.engs[e].wait_ge(self.sem("bar"), self.barc)
        self.lastw = {}
        self.readers = {}
        self.nins += 6


def _consts(T):
    c = {}
    c["ident"] = np.eye(128, dtype=np.float32)
    m = np.arange(128)[:, None]
    l = np.arange(128)[None, :]
    same = (m // 64) == (l // 64)
    c["trix_f"] = (same & (m <= l)).astype(np.float32) / 16.0
    c["tris_f"] = (same & (m > l)).astype(np.float32) / 16.0
    c["trix_b"] = (same & (m >= l)).astype(np.float32) / 16.0
    c["tris_b"] = (same & (m < l)).astype(np.float32) / 16.0
    c["mask_f"] = np.repeat((same & (m <= l)).astype(np.float32)[:, None, :], 4, axis=1).copy()
    c["mask_b"] = np.repeat((same & (m >= l)).astype(np.float32)[:, None, :], 4, axis=1).copy()
    Tv = 384
    bm = np.zeros((5, 4, 128, 128), np.float32)
    for g, wd in enumerate((2, 4, 8, 16)):
        A = np.zeros((Tv, Tv), np.float32)
        for t in range(Tv):
            lo = min(max(t - wd // 2, 0), Tv)
            hi = min(max(t + wd - wd // 2, 0), Tv)
            A[t, lo:hi] = 1.0 / float(hi - lo)
            A[t, t] -= 1.0
        bm[0, g] = A[128:256, 0:128].T
        bm[1, g] = A[128:256, 128:256].T
        bm[2, g] = A[128:256, 256:384].T
        bm[3, g] = A[0:128, 0:128].T
        bm[4, g] = A[256:384, 256:384].T
    c["bmat"] = np.ascontiguousarray(bm.transpose(2, 0, 1, 3))
    c["negh"] = np.full((128, 4), -0.5, np.float32)
    return c


def _na_tiles(NB):
    def chunks(i):
        if i < 2:
            return [0, 1, 2, 3]
        if i > NB - 3:
            return [NB - 4, NB - 3, NB - 2, NB - 1]
        return [i - 2, i - 1, i, i + 1, i + 2]

    def tile_base(i):
        if i == 0:
            return 5
        if i == 1:
            return 9
        if i == NB - 2:
            return 13
        if i == NB - 1:
            return 17
        return 0
    return chunks, tile_base


def _na_bias_index(NB):
    rows = NB * 2
    chunks, tile_base = _na_tiles(NB)
    ri = np.zeros((21, 128, 128), np.int64)
    ci = np.zeros((21, 128, 128), np.int64)
    va = np.zeros((21, 128, 128), bool)
    kk = np.arange(128)[:, None]
    qq = np.arange(128)[None, :]
    for i in (2, 0, 1, NB - 2, NB - 1):
        for n, j in enumerate(chunks(i)):
            t = tile_base(i) + n
            kr = 2 * j + kk // 64
            kc = kk % 64
            qr = 2 * i + qq // 64
            qc = qq % 64
            r0 = np.clip(qr - 4, 0, rows - 8)
            c0 = np.clip(qc - 8, 0, 48)
            v = (kr >= r0) & (kr < r0 + 8) & (kc >= c0) & (kc < c0 + 16)
            ri[t] = np.where(v, kr - qr + 7, 0)
            ci[t] = np.where(v, kc - qc + 15, 0)
            va[t] = v
    return ri, ci, va


def build(NS, T, L):
    NT = NS * T
    NB = T // 128
    NTL = T // 512
    NTILE = NT // 512
    nc = bass.Bass("TRN2", target_bir_lowering=False)

    def din(name, shape, dt=F32):
        return nc.dram_tensor(name, list(shape), dt, kind="ExternalInput").ap()

    def dscr(name, shape, dt):
        return nc.dram_tensor(name, list(shape), dt).ap()

    x_in = din("x", [NT, D])
    y_out = nc.dram_tensor("y", [NT, D], F32, kind="ExternalOutput").ap()
    w_in = din("w_in", [L, D, DIN])
    w_bra = din("w_br_a", [L, 512, D])
    w_brb = din("w_br_b", [L, 256, D])
    w_brc = din("w_br_c", [L, 256, D])
    w_out = din("w_out", [L, D, D])
    w_gate = din("w_gate", [L, D, DFF])
    w_up = din("w_up", [L, D, DFF])
    w_down = din("w_down", [L, DFF, D])
    upx = din("upx", [L, 2, 17, 256])
    gnorm = din("gla_norm", [L, 128])
    pool_w = din("pool_w", [L, 4, 64, 64])
    pool_sc = din("pool_scale", [L, 128, 2])
    nabias = din("nabias", [L, 128, 21, 4, 128])
    lnp = din("lnp", [L, 4, D])
    c_ident = din("ident", [128, 128])
    c_tri = din("tri", [4, 128, 128])
    c_mask = din("mask", [2, 128, 4, 128])
    c_bmat = din("bmat", [128, 5, 4, 128])
    c_negh = din("negh", [128, 4])

    X1 = dscr("X1", [NT, D], F32)
    X2 = dscr("X2", [NT, D], F32)
    s_qT = dscr("s_qT", [NS, 256, T], BF16)
    s_kT = dscr("s_kT", [NS, 256, T], BF16)
    s_lrT = dscr("s_lrT", [NS, 32, T], F32)
    s_nqT = dscr("s_nqT", [NS, 256, T], BF16)
    s_nkT = dscr("s_nkT", [NS, 256, T], BF16)
    s_k = dscr("s_k", [NT, 256], BF16)
    s_v = dscr("s_v", [NT, 512], BF16)
    s_og = dscr("s_og", [NT, 512], BF16)
    s_p = dscr("s_p", [NT, 256], BF16)
    s_nv = dscr("s_nv", [NT, 260], BF16)
    s_gT = dscr("s_gT", [NS, 512, T], BF16)
    s_pbT = dscr("s_pbT", [NS, 256, T], BF16)
    s_naT = dscr("s_naT", [NS, 256, T], BF16)
    s_of = dscr("s_of", [NT, 512], F32)

    S = Sched(nc)
    op = S.op
    glob = ExitStack()

    uid = {"n": 0}

    def sbt(st, name, shape, dt):
        uid["n"] += 1
        return st.enter_context(nc.sbuf_tensor(f"sb{uid['n']}_{name}", list(shape), dt))

    PS = [glob.enter_context(nc.psum_tensor(f"psb{i}", [128, 512], F32)) for i in range(8)]
    ident = sbt(glob, "ident", [128, 128], BF16)
    op("pool", lambda e: e.dma_start(out=ident[:], in_=c_ident), w=["ident"], dsem="c_const")

    rr = {"ev": 0, "bank": 0}

    def evac(out_ap, in_ap, r, w, func=None, scale=None, eng=None):
        if func is not None or scale is not None:
            eng = "act"
        if eng is None:
            eng = "act" if rr["ev"] % 2 == 0 else "dve"
            rr["ev"] += 1
        if eng == "act":
            f = func if func is not None else AF.Copy
            if scale is not None:
                return op("act", lambda e: e.activation(out=out_ap, in_=in_ap, func=f, scale=scale), r=r, w=w)
            return op("act", lambda e: e.activation(out=out_ap, in_=in_ap, func=f), r=r, w=w)
        return op(eng, lambda e: e.tensor_copy(out=out_ap, in_=in_ap), r=r, w=w)

    def mm(out_ap, lhsT, rhs, start, stop, r, w, sig=None, skip=False):
        if sig is None:
            sig = stop
        if skip:
            return op("pe", lambda e: e.matmul(out_ap, lhsT, rhs, start=start, stop=stop, skip_group_check=True),
                      r=r, w=w, sig=sig)
        return op("pe", lambda e: e.matmul(out_ap, lhsT, rhs, start=start, stop=stop), r=r, w=w, sig=sig)

    def load_w(dst, src_rows, kchunks, dsem, key):
        for k in range(kchunks):
            op("pool", lambda e, k=k: e.dma_start(out=dst[:, k, :], in_=src_rows[k * 128:(k + 1) * 128, :]),
               w=[key], dsem=dsem)

    def make_xT(xblk_ap, xkey, xbf, xbfkey, xT, xTkey, b, bank, ceng="pool"):
        if ceng is None:
            pass
        elif ceng == "act":
            op("act", lambda e: e.activation(out=xbf[:], in_=xblk_ap, func=AF.Copy), r=[xkey], w=[xbfkey])
        else:
            op("pool", lambda e: e.tensor_copy(out=xbf[:], in_=xblk_ap), r=[xkey], w=[xbfkey])
        pk = f"ps{bank}"
        pv = PS[bank][:].bitcast(BF16).rearrange("p (k t) -> p k t", k=8)
        for k in range(8):
            op("pe", lambda e, k=k: e.transpose(pv[:, k, :], xbf[:, k * 128:(k + 1) * 128], ident[:]),
               r=[xbfkey, "ident"], w=[pk], sig=(k == 7))
        evac(xT[:, :, b * 128:(b + 1) * 128], pv, r=[pk], w=[xTkey])

    def nextbank(n=8, base=0):
        b = base + rr["bank"] % n
        rr["bank"] += 1
        return b

    def layer_norm(st_, xb, xkey, g_t, b_t, tagp, stats, mv, rstd):
        op("dve", lambda e: e.bn_stats(out=stats[:, 0:6], in_=xb[:, 0:512]), r=[xkey], w=[tagp + "st"])
        op("dve", lambda e: e.bn_stats(out=stats[:, 6:12], in_=xb[:, 512:1024]), r=[xkey], w=[tagp + "st"])
        yield
        op("dve", lambda e: e.bn_aggr(out=mv[:, 0:2], in_=stats[:, 0:12]), r=[tagp + "st"], w=[tagp + "mv"])
        op("dve", lambda e: e.tensor_scalar(out=mv[:, 2:3], in0=mv[:, 1:2], scalar1=LN_EPS, scalar2=None,
                                            op0=ALU.add), r=[tagp + "mv"], w=[tagp + "mv2"])
        yield
        op("pool", lambda e: e.tensor_tensor(out=rstd[:, 0:1], in0=mv[:, 2:3], in1=negh[:, 0:1], op=ALU.pow),
           r=[tagp + "mv2", "negh"], w=[tagp + "rs"])
        yield
        op("dve", lambda e: e.tensor_scalar(out=mv[:, 3:4], in0=mv[:, 0:1], scalar1=rstd[:, 0:1], scalar2=-1.0,
                                            op0=ALU.mult, op1=ALU.mult), r=[tagp + "mv", tagp + "rs"], w=[tagp + "nm"])
        yield
        op("act", lambda e: e.activation(out=xb, in_=xb, func=AF.Identity, bias=mv[:, 3:4], scale=rstd[:, 0:1]),
           r=[xkey, tagp + "nm", tagp + "rs"], w=[xkey])
        yield
        op("dve", lambda e: e.tensor_tensor(out=xb, in0=xb, in1=g_t[:], op=ALU.mult), r=[xkey, "lnp"], w=[xkey])
        yield
        op("pool", lambda e: e.tensor_tensor(out=xb, in0=xb, in1=b_t[:], op=ALU.add), r=[xkey, "lnp"], w=[xkey])
        yield

    def run_rr(gens):
        gens = list(gens)
        while gens:
            gens = [g for g in gens if next(g, "done") != "done"]

    class Deferred:
        def __init__(self):
            self.gens = []
            self.after = None

        def set(self, gens, after=None):
            self.gens = list(gens)
            self.after = after

        def advance(self, n):
            for _ in range(n):
                if not self.gens:
                    break
                g = self.gens.pop(0)
                if next(g, "done") != "done":
                    self.gens.append(g)
            if not self.gens and self.after is not None:
                f, self.after = self.after, None
                f()

        def drain(self):
            while self.gens or self.after is not None:
                self.advance(64)

    negh = sbt(glob, "negh", [128, 4], F32)
    op("sp", lambda e: e.dma_start(out=negh[:], in_=c_negh), w=["negh"], dsem="c_negh")

    def phase_A(l, xsrc):
        with ExitStack() as st:
            W = sbt(st, "A_w", [128, 8, NMIX], BF16)
            load_w(W, w_in[l][:, 0:NMIX], 8, "A_w", "A_w")
            xblk = [sbt(st, f"A_x{i}", [128, D], F32) for i in range(8)]
            xbf = [sbt(st, f"A_xbf{i}", [128, D], BF16) for i in range(4)]
            xT = [sbt(st, f"A_xT{i}", [128, 8, 512], BF16) for i in range(2)]
            stF = [sbt(st, f"A_stF{i}", [128, 8, 512], BF16) for i in range(2)]
            stL = [sbt(st, f"A_stL{i}", [32, 512], F32) for i in range(2)]
            stT = [sbt(st, f"A_stT{i}", [128, 4, 1536], BF16) for i in range(2)]
            stNV = [sbt(st, f"A_stNV{i}", [128, 4, 4, 65], BF16) for i in range(2)]
            for i in range(2):
                op("pool", lambda e, i=i: e.memset(stNV[i][:], 1.0), w=[f"A_stNV{i}"])
            FMG = [(0, 128, 0.125, s_qT, 0), (128, 128, 0.125, s_qT, 128), (256, 128, None, s_kT, 0),
                   (384, 128, None, s_kT, 128), (1824, 128, 0.125, s_nqT, 0), (1952, 128, 0.125, s_nqT, 128),
                   (2080, 128, None, s_nkT, 0), (2208, 128, None, s_nkT, 128)]
            TMG = [(256, 256, 0, None), (512, 512, 256, None), (1024, 512, 768, AF.Silu), (1568, 256, 1280, None)]

            def loads(ti):
                par = ti % 2
                for b in range(4):
                    i = par * 4 + b
                    op("sp", lambda e, i=i, b=b: e.dma_start(out=xblk[i][:], in_=xsrc[ti * 512 + b * 128: ti * 512 + (b + 1) * 128, :]),
                       w=[f"A_x{i}"], dsem=f"A_x{i}")

            def mk_xT(tn):
                pn = tn % 2
                for b in range(4):
                    op("act", lambda e, b=b: e.activation(out=xbf[b][:], in_=xblk[pn * 4 + b][:], func=AF.Copy),
                       r=[f"A_x{pn * 4 + b}"], w=[f"A_xbf{b}"])
                for b in range(4):
                    make_xT(None, None, xbf[b], f"A_xbf{b}", xT[pn], f"A_xT{pn}", b, nextbank(), ceng=None)

            loads(0)
            mk_xT(0)
            for ti in range(NTILE):
                par = ti % 2
                s, t0 = divmod(ti, NTL)
                t0 *= 512
                if ti + 1 < NTILE:
                    loads(ti + 1)
                xk = f"A_xT{par}"
                for gi, (c0, ncol, scale, scr, r0) in enumerate(FMG):
                    bk = nextbank()
                    for k in range(8):
                        mm(PS[bk][0:ncol, :], W[:, k, c0:c0 + ncol], xT[par][:, k, :], k == 0, k == 7,
                           r=["A_w", xk], w=[f"ps{bk}"])
                    evac(stF[par][0:ncol, gi, :], PS[bk][0:ncol, :], r=[f"ps{bk}"], w=[f"A_stF{par}"], scale=scale)
                if ti + 1 < NTILE:
                    mk_xT(ti + 1)
                bk = nextbank()
                for k in range(8):
                    mm(PS[bk][0:32, :], W[:, k, 1536:1568], xT[par][:, k, :], k == 0, k == 7, r=["A_w", xk], w=[f"ps{bk}"])
                evac(stL[par][:], PS[bk][0:32, :], r=[f"ps{bk}"], w=[f"A_stL{par}"])
                for b in range(4):
                    for (c0, ncol, off, func) in TMG:
                        bk = nextbank()
                        for k in range(8):
                            mm(PS[bk][:, 0:ncol], xT[par][:, k, b * 128:(b + 1) * 128], W[:, k, c0:c0 + ncol], k == 0, k == 7,
                               r=["A_w", xk], w=[f"ps{bk}"])
                        evac(stT[par][:, b, off:off + ncol], PS[bk][:, 0:ncol], r=[f"ps{bk}"], w=[f"A_stT{par}"], func=func)
                    bk = nextbank()
                    for k in range(8):
                        mm(PS[bk][:, 0:256], xT[par][:, k, b * 128:(b + 1) * 128], W[:, k, 2336:2592], k == 0, k == 7,
                           r=["A_w", xk], w=[f"ps{bk}"])
                    evac(stNV[par][:, b, :, 0:64], PS[bk][:, 0:256].rearrange("p (h d) -> p h d", h=4),
                         r=[f"ps{bk}"], w=[f"A_stNV{par}"])
                for gi, (c0, ncol, scale, scr, r0) in enumerate(FMG):
                    op("sp", lambda e, gi=gi, scr=scr, r0=r0: e.dma_start(out=scr[s, r0:r0 + 128, t0:t0 + 512], in_=stF[par][:, gi, :]),
                       r=[f"A_stF{par}"], dsem=f"A_stF{par}")
                op("sp", lambda e: e.dma_start(out=s_lrT[s, :, t0:t0 + 512], in_=stL[par][:]), r=[f"A_stL{par}"], dsem=f"A_stL{par}")
                rows = slice(ti * 512, ti * 512 + 512)
                for (scr, off, ncol) in ((s_k, 0, 256), (s_v, 256, 512), (s_og, 768, 512), (s_p, 1280, 256)):
                    op("sp", lambda e, scr=scr, off=off, ncol=ncol: e.dma_start(
                        out=scr[rows, :].rearrange("(b p) c -> p b c", p=128), in_=stT[par][:, :, off:off + ncol]),
                       r=[f"A_stT{par}"], dsem=f"A_stT{par}")
                op("sp", lambda e: e.dma_start(out=s_nv[rows, :].rearrange("(b p) c -> p b c", p=128),
                                               in_=stNV[par][:].rearrange("p b h d -> p b (h d)")),
                   r=[f"A_stNV{par}"], dsem=f"A_stNV{par}")
            S.barrier()

    def phase_B(l):
        chunks_of, tile_base = _na_tiles(NB)
        with ExitStack() as st:
            tri = sbt(st, "B_cst", [128, 4, 128], F32)
            for i in range(4):
                op("sp", lambda e, i=i: e.dma_start(out=tri[:, i, :], in_=c_tri[i]), w=["B_cst"], dsem="B_cst")
            msk = sbt(st, "B_cst", [128, 2, 4, 128], F32)
            for i in range(2):
                op("sp", lambda e, i=i: e.dma_start(out=msk[:, i], in_=c_mask[i]), w=["B_cst"], dsem="B_cst")
            upw = sbt(st, "B_cst", [17, 2, 256], F32)
            for i in range(2):
                op("sp", lambda e, i=i: e.dma_start(out=upw[:, i, :], in_=upx[l, i]), w=["B_cst"], dsem="B_cst")
            gnb = sbt(st, "B_cst", [128, 128], F32)
            op("sp", lambda e: e.dma_start(out=gnb[:], in_=gnorm[l].partition_broadcast(128)), w=["B_cst"], dsem="B_cst")
            bmat = sbt(st, "B_bmat", [128, 5, 4, 128], BF16)
            op("pool", lambda e: e.dma_start(out=bmat[:], in_=c_bmat), w=["B_bmat"], dsem="B_bmat")
            pwbd = sbt(st, "B_pwbd", [128, 2, 128], F32)
            pwbf = sbt(st, "B_pwbf", [128, 2, 128], BF16)
            op("pool", lambda e: e.memset(pwbd[:], 0.0), w=["B_pwbd"])
            for g in range(4):
                pb = (g % 2) * 64
                op("sp", lambda e, g=g, pb=pb: e.dma_start(out=pwbd[pb:pb + 64, g // 2, pb:pb + 64], in_=pool_w[l, g]),
                   w=["B_pwbd"], r=[], dsem="B_pwbd")
            op("pool", lambda e: e.tensor_copy(out=pwbf[:], in_=pwbd[:]), r=["B_pwbd"], w=["B_pwbf"])
            psc = sbt(st, "B_cst", [128, 2], F32)
            op("sp", lambda e: e.dma_start(out=psc[:], in_=pool_sc[l]), w=["B_cst"], dsem="B_cst")
            nab = sbt(st, "B_nab", [128, 21, 4, 128], BF16)
            oft = [sbt(st, f"B_of{i}", [128, 4, 512], F32) for i in range(2)]
            nq = sbt(st, "B_nq", [128, 2, T], BF16)
            nk = sbt(st, "B_nk", [128, 2, T], BF16)
            nvx = sbt(st, "B_nvx", [128, NB, 260], BF16)
            pT = nq[:].rearrange("p a t -> p (a t)").rearrange("p (b c) -> p b c", c=256)
            qTt = [sbt(st, f"B_qT{i}", [128, 2, 512], BF16) for i in range(2)]
            kTt = [sbt(st, f"B_kT{i}", [128, 2, 512], BF16) for i in range(2)]
            kt = [sbt(st, f"B_k{i}", [128, 4, 256], BF16) for i in range(2)]
            vt = [sbt(st, f"B_v{i}", [128, 4, 512], BF16) for i in range(2)]
            ogt = [sbt(st, f"B_og{i}", [128, 4, 512], BF16) for i in range(2)]
            lrt = [sbt(st, f"B_lr{i}", [17, 512], F32) for i in range(2)]
            for i in range(2):
                op("pool", lambda e, i=i: e.memset(lrt[i][:], 1.0), w=[f"B_lr{i}"])
            e_sb = sbt(st, "B_e", [128, 2, 256], F32)
            c_sb = sbt(st, "B_c", [128, 2, 256], F32)
            Eq = [sbt(st, f"B_Eq{i}", [128, 2, 256], F32) for i in range(2)]
            Ek = sbt(st, "B_Ek", [128, 2, 256], F32)
            ER = sbt(st, "B_ER", [128, 2, 256], F32)
            qd = [sbt(st, f"B_qd{i}", [128, 2, 256], BF16) for i in range(2)]
            ki = [sbt(st, f"B_ki{i}", [128, 2, 256], BF16) for i in range(2)]
            ke = [sbt(st, f"B_ke{i}", [128, 2, 256], BF16) for i in range(2)]
            attm = sbt(st, "B_attm", [128, 2, 2, 128], BF16)
            S32 = [sbt(st, f"B_S32{i}", [128, 2, 256], F32) for i in range(2)]
            Sbf = [sbt(st, f"B_Sbf{i}", [128, 2, 256], BF16) for i in range(4)]
            o_sb = sbt(st, "B_o", [128, 512], F32)
            sq = sbt(st, "B_sq", [128, 512], F32)
            ss = sbt(st, "B_ss", [128, 4], F32)
            rs = sbt(st, "B_rs", [128, 4], F32)
            ogn = [sbt(st, f"B_ogn{i}", [128, 4, 512], F32) for i in range(2)]
            gnb4 = sbt(st, "B_gnb4", [128, 512], F32)
            for h_ in range(4):
                op("pool", lambda e, h_=h_: e.tensor_copy(out=gnb4[:, h_ * 128:(h_ + 1) * 128], in_=gnb[:]), r=["B_cst"], w=["B_gnb4"])
            gl = sbt(st, "B_gl", [128, 512], BF16)
            gst = [sbt(st, f"B_gst{i}", [128, 4, 512], BF16) for i in range(2)]
            uT = sbt(st, "B_uT", [128, 2, 512], BF16)
            pbst = [sbt(st, f"B_pbst{i}", [128, 2, 512], BF16) for i in range(2)]
            PT = [sbt(st, f"B_PT{i}", [128, 5, 128], BF16) for i in range(2)]
            rcp = sbt(st, "B_rcp", [128, 4], F32)
            na_tm = sbt(st, "B_natm", [128, 256], BF16)
            nast = [sbt(st, f"B_nast{i}", [128, 2, 512], BF16) for i in range(2)]

            BK_Z, BK_ATT, BK_ATO, BK_O, BK_DS0, BK_DS1, BK_NA0, BK_NA1 = range(8)
            KZ, KATT, KATO, KNA0, KNA1 = "psZ", "psATT", "psATO", "psNA0", "psNA1"

            def gla_tile_loads(s, tl, dirn, par):
                t0 = tl * 512
                rows = slice(s * T + t0, s * T + t0 + 512)
                op("sp", lambda e: e.dma_start(out=qTt[par][:], in_=s_qT[s].rearrange("(p d) t -> d p t", d=128)[:, :, t0:t0 + 512]),
                   w=[f"B_qT{par}"], dsem=f"B_qT{par}")
                op("sp", lambda e: e.dma_start(out=kTt[par][:], in_=s_kT[s].rearrange("(p d) t -> d p t", d=128)[:, :, t0:t0 + 512]),
                   w=[f"B_kT{par}"], dsem=f"B_kT{par}")
                op("sp", lambda e: e.dma_start(out=kt[par][:], in_=s_k[rows, :].rearrange("(b p) c -> p b c", p=128)),
                   w=[f"B_k{par}"], dsem=f"B_k{par}")
                op("sp", lambda e: e.dma_start(out=vt[par][:], in_=s_v[rows, :].rearrange("(b p) c -> p b c", p=128)),
                   w=[f"B_v{par}"], dsem=f"B_v{par}")
                op("sp", lambda e: e.dma_start(out=lrt[par][0:16, :], in_=s_lrT[s, dirn * 16:(dirn + 1) * 16, t0:t0 + 512]),
                   w=[f"B_lr{par}"], dsem=f"B_lr{par}")
                if dirn == 1:
                    op("sp", lambda e: e.dma_start(out=ogt[par][:], in_=s_og[rows, :].rearrange("(b p) c -> p b c", p=128)),
                       w=[f"B_og{par}"], dsem=f"B_og{par}")
                    op("sp", lambda e: e.dma_start(out=oft[par][:], in_=s_of[rows, :].rearrange("(b p) c -> p b c", p=128)),
                       r=[f"dr_of_{s}_{tl}"], w=[f"B_of{par}"], dsem=f"B_of{par}")
                    for b_ in range(4):
                        op("pool", lambda e, b_=b_: e.tensor_tensor(out=ogn[par][:, b_, :], in0=ogt[par][:, b_, :], in1=gnb4[:], op=ALU.mult),
                           r=[f"B_og{par}", "B_gnb4"], w=[f"B_ogn{par}"])

            def gla_pass(s, dirn):
                NU = T // 256
                units = list(range(NU)) if dirn == 0 else list(range(NU - 1, -1, -1))
                for i_ in range(2):
                    op("pool", lambda e, i_=i_: e.memset(S32[i_][:], 0.0), w=[f"B_S32{i_}"])
                for c in range(2):
                    op("dve", lambda e, c=c: e.memset(PS[BK_DS0 + c][:], 0.0), w=[f"psDS{c}"])
                ring = {"i": 0}
                tiles_order = []
                for u in units:
                    if not tiles_order or tiles_order[-1] != u // 2:
                        tiles_order.append(u // 2)
                tpar = {tl: n % 2 for n, tl in enumerate(tiles_order)}
                gla_tile_loads(s, tiles_order[0], dirn, 0)

                def uparams(ui):
                    u = units[ui]
                    tl, uh = divmod(u, 2)
                    return u, tl, uh, tpar[tl], ui % 2, slice(uh * 256, uh * 256 + 256)

                def s1(ui):
                    u, tl, uh, par, q1, tu = uparams(ui)
                    zv = PS[BK_Z][:].rearrange("p (j c) -> p j c", j=2)
                    for j in range(2):
                        c0 = uh * 256 + j * 128
                        mm(zv[:, j, :], lrt[par][0:17, c0:c0 + 128], upw[0:17, dirn, :], True, True,
                           r=[f"B_lr{par}", "B_cst"], w=["psZ"], sig=(j == 1))
                    yield "s1"
                    op("act", lambda e: e.activation(out=e_sb[:], in_=zv, func=AF.Exp, scale=-1.0), r=["psZ"], w=["B_e"])
                    op("act", lambda e: e.activation(out=c_sb[:], in_=e_sb[:], func=AF.Ln, bias=1.0), r=["B_e"], w=["B_c"])
                    yield "s1"
                    btv = PS[BK_Z][:].rearrange("p (a t) -> p a t", a=2)
                    for j in range(2):
                        for p in range(2):
                            mm(btv[:, p, j * 128:(j + 1) * 128], c_sb[:, j, p * 128:(p + 1) * 128], tri[:, dirn * 2, :], True, True,
                               r=["B_c", "B_cst"], w=["psZ"], sig=(j == 1 and p == 1))
                    yield "s1"
                    op("act", lambda e: e.activation(out=Eq[q1][:], in_=btv, func=AF.Exp, scale=-1.0), r=["psZ"], w=[f"B_Eq{q1}"])
                    op("act", lambda e: e.activation(out=Ek[:], in_=btv, func=AF.Exp, scale=1.0), r=["psZ"], w=["B_Ek"])
                    yield "s1"
                    rv = PS[BK_Z][:].rearrange("p (j c) -> p j c", j=2)
                    for j in range(2):
                        mm(rv[:, j, :], tri[:, dirn * 2 + 1, :], c_sb[:, j, :], True, True, r=["B_c", "B_cst"], w=["psZ"], sig=(j == 1))
                    yield "s1"
                    op("act", lambda e: e.activation(out=ER[:], in_=rv, func=AF.Exp, scale=-1.0), r=["psZ"], w=["B_ER"])
                    yield "s1"
                    op("dve", lambda e: e.tensor_tensor(out=qd[q1][:], in0=qTt[par][:, :, tu], in1=Eq[q1][:], op=ALU.mult),
                       r=[f"B_qT{par}", f"B_Eq{q1}"], w=[f"B_qd{q1}"])
                    op("dve", lambda e: e.tensor_tensor(out=ki[q1][:], in0=kTt[par][:, :, tu], in1=Ek[:], op=ALU.mult),
                       r=[f"B_kT{par}", "B_Ek"], w=[f"B_ki{q1}"])
                    op("dve", lambda e: e.tensor_tensor(out=ke[q1][:], in0=kt[par][:, uh * 2:uh * 2 + 2, :], in1=ER[:], op=ALU.mult),
                       r=[f"B_k{par}", "B_ER"], w=[f"B_ke{q1}"])
                    yield "s1"

                def s2(ui):
                    u, tl, uh, par, q1, tu = uparams(ui)
                    n = tiles_order.index(tl)
                    first_of_tile = (ui == 0) or (units[ui - 1] // 2 != tl)
                    if first_of_tile and n + 1 < len(tiles_order):
                        gla_tile_loads(s, tiles_order[n + 1], dirn, 1 - par)
                    jorder = [0, 1] if dirn == 0 else [1, 0]
                    corder = [0, 1] if dirn == 0 else [1, 0]
                    for j in jorder:
                        b = uh * 2 + j
                        tb = slice(b * 128, (b + 1) * 128)
                        tj = slice(j * 128, (j + 1) * 128)
                        for c in range(2):
                            dsv = PS[BK_DS0 + c][:].rearrange("p (a t) -> p a t", a=2)
                            for h in range(4):
                                pb = (h % 2) * 64
                                cs = slice((h % 2) * 128, (h % 2) * 128 + 128)
                                mm(dsv[pb:pb + 64, h // 2, cs], ke[q1][c * 64:(c + 1) * 64, j, h * 64:(h + 1) * 64],
                                   vt[par][c * 64:(c + 1) * 64, b, h * 128:(h + 1) * 128], True, True,
                                   r=[f"B_ke{q1}", f"B_v{par}"], w=[f"psDS{c}"], sig=(h == 3))
                        snaps = {}
                        for c in corder:
                            k = ring["i"] % 4
                            cur = ring["i"] % 2
                            ring["i"] += 1
                            snaps[c] = k
                            op("act", lambda e, k=k, cur=cur: e.activation(out=Sbf[k][:], in_=S32[cur][:], func=AF.Copy),
                               r=[f"B_S32{cur}"], w=[f"B_Sbf{k}"])
                            col = j * 128 + c * 64 + (63 if dirn == 0 else 0)
                            dsv = PS[BK_DS0 + c][:].rearrange("p (a t) -> p a t", a=2)
                            for p in range(2):
                                op("dve", lambda e, p=p, col=col, cur=cur, dsv=dsv: e.scalar_tensor_tensor(
                                    out=S32[1 - cur][:, p, :], in0=S32[cur][:, p, :], scalar=Eq[q1][:, p, col:col + 1],
                                    in1=dsv[:, p, :], op0=ALU.mult, op1=ALU.add),
                                   r=[f"B_S32{cur}", f"B_Eq{q1}", f"psDS{c}"], w=[f"B_S32{1 - cur}"])
                        att_e = PS[BK_ATT][:, 0:256].rearrange("p (a t) -> p a t", a=2)
                        att_o = PS[BK_ATO][:, 0:256].rearrange("p (a t) -> p a t", a=2)
                        for h in (0, 2):
                            mm(att_e[:, h // 2, :], ki[q1][0:64, h // 2, tj], qd[q1][0:64, h // 2, tj], True, True,
                               r=[f"B_ki{q1}", f"B_qd{q1}"], w=["psATT"], sig=(h == 2))
                        for h in (1, 3):
                            mm(att_o[:, h // 2, :], ki[q1][64:128, h // 2, tj], qd[q1][64:128, h // 2, tj], True, True,
                               r=[f"B_ki{q1}", f"B_qd{q1}"], w=["psATO"], sig=(h == 3))
                        op("dve", lambda e: e.tensor_tensor(out=attm[:, 0], in0=att_e, in1=msk[:, dirn, 0:2, :], op=ALU.mult),
                           r=["psATT", "B_cst"], w=["B_attm"])
                        op("dve", lambda e: e.tensor_tensor(out=attm[:, 1], in0=att_o, in1=msk[:, dirn, 0:2, :], op=ALU.mult),
                           r=["psATO", "B_cst"], w=["B_attm"])
                        yield "sub"
                        for h in range(4):
                            mm(PS[BK_O][:, h * 128:(h + 1) * 128], attm[:, h % 2, h // 2, :], vt[par][:, b, h * 128:(h + 1) * 128],
                               h == 0, False, r=["B_attm", f"B_v{par}"], w=["psO"], sig=False, skip=True)
                        for ci, c in enumerate(corder):
                            k = snaps[c]
                            for p in range(2):
                                mm(PS[BK_O][c * 64:(c + 1) * 64, p * 256:(p + 1) * 256], qd[q1][:, p, j * 128 + c * 64: j * 128 + (c + 1) * 64],
                                   Sbf[k][:, p, :], False, (ci == 1 and p == 1), r=[f"B_qd{q1}", f"B_Sbf{k}"], w=["psO"],
                                   sig=(p == 1), skip=True)
                        yield "sub"
                        if dirn == 0:
                            evac(oft[par][:, b, :], PS[BK_O][:], r=["psO"], w=[f"B_of{par}"], eng="act")
                        else:
                            op("dve", lambda e: e.tensor_tensor(out=o_sb[:], in0=PS[BK_O][:], in1=oft[par][:, b, :], op=ALU.add),
                               r=["psO", f"B_of{par}"], w=["B_o"])
                            op("act", lambda e: e.activation(out=sq[:], in_=o_sb[:], func=AF.Square), r=["B_o"], w=["B_sq"])
                            op("dve", lambda e: e.tensor_reduce(out=ss[:], in_=sq[:].rearrange("p (h d) -> p h d", h=4),
                                                                axis=AX.X, op=ALU.add), r=["B_sq"], w=["B_ss"])
                            op("dve", lambda e: e.tensor_scalar(out=ss[:], in0=ss[:], scalar1=1.0 / 128.0, scalar2=RMS_EPS,
                                                                op0=ALU.mult, op1=ALU.add), r=["B_ss"], w=["B_ss"])
                            op("act", lambda e: e.activation(out=rs[:], in_=ss[:], func=AF.Ln), r=["B_ss"], w=["B_rs"])
                            op("act", lambda e: e.activation(out=rs[:], in_=rs[:], func=AF.Exp, scale=-0.5), r=["B_rs"], w=["B_rs"])
                            for h in range(4):
                                op("dve", lambda e, h=h: e.scalar_tensor_tensor(
                                    out=gl[:, h * 128:(h + 1) * 128], in0=o_sb[:, h * 128:(h + 1) * 128], scalar=rs[:, h:h + 1],
                                    in1=ogn[par][:, b, h * 128:(h + 1) * 128], op0=ALU.mult, op1=ALU.mult),
                                   r=["B_o", "B_rs", f"B_ogn{par}"], w=["B_gl"])
                            trv = PS[BK_ATT][:].bitcast(BF16)[:, 512:1024].rearrange("p (k t) -> p k t", k=4)
                            for k4 in range(4):
                                op("pe", lambda e, k4=k4: e.transpose(trv[:, k4, :], gl[:, k4 * 128:(k4 + 1) * 128], ident[:]),
                                   r=["B_gl", "ident"], w=["psATT"], sig=(k4 == 3))
                            evac(gst[par][:, :, tb], trv, r=["psATT"], w=[f"B_gst{par}"])
                        yield "blk"
                    last_of_tile = (ui == len(units) - 1) or (units[ui + 1] // 2 != tl)
                    if last_of_tile:
                        if dirn == 0:
                            rws = slice(s * T + tl * 512, s * T + tl * 512 + 512)
                            op("sp", lambda e: e.dma_start(out=s_of[rws, :].rearrange("(b p) c -> p b c", p=128), in_=oft[par][:]),
                               r=[f"B_of{par}"], w=[f"dr_of_{s}_{tl}"], dsem=f"B_ofs{par}")
                        else:
                            t0 = tl * 512
                            op("sp", lambda e: e.dma_start(out=s_gT[s].rearrange("(c p) t -> p c t", p=128)[:, :, t0:t0 + 512],
                                                           in_=gst[par][:]), r=[f"B_gst{par}"], dsem=f"B_gst{par}")


                for _ in s1(0):
                    pass
                for ui in range(len(units)):
                    g1 = s1(ui + 1) if ui + 1 < len(units) else iter(())
                    g2 = s2(ui)
                    a1 = a2 = True
                    while a1 or a2:
                        if a2:
                            ev = next(g2, None)
                            if ev is None:
                                a2 = False
                            else:
                                yield ev
                        if a1:
                            if next(g1, None) is None:
                                a1 = False

            def seq_loads(s):
                rows = slice(s * T, (s + 1) * T)
                op("sp", lambda e: e.dma_start(out=pT, in_=s_p[rows, :].rearrange("(b p) c -> p b c", p=128)), w=["B_nq"], dsem="B_nq")
                op("sp", lambda e: e.dma_start(out=nk[:], in_=s_nkT[s].rearrange("(p d) t -> d p t", d=128)), w=["B_nk"], dsem="B_nk")
                op("sp", lambda e: e.dma_start(out=nvx[:], in_=s_nv[rows, :].rearrange("(b p) c -> p b c", p=128)), w=["B_nvx"], dsem="B_nvx")

            def seq_loads_bwd(s):
                op("sp", lambda e: e.dma_start(out=nq[:], in_=s_nqT[s].rearrange("(p d) t -> d p t", d=128)), w=["B_nq"], dsem="B_nq")

            def pool_block(s, i):
                bku = BK_NA0
                uv = PS[bku][:, 0:256].rearrange("p (a t) -> p a t", a=2)
                ds_ = [d for d in (-1, 0, 1) if 0 <= i + d < NB]
                for g in range(4):
                    pb = (g % 2) * 64
                    for n, d in enumerate(ds_):
                        var = d + 1
                        if d == 0 and i == 0:
                            var = 3
                        if d == 0 and i == NB - 1:
                            var = 4
                        mm(uv[pb:pb + 64, g // 2, :], pT[:, i + d, g * 64:(g + 1) * 64], bmat[:, var, g, :],
                           n == 0, n == len(ds_) - 1, r=["B_nq", "B_bmat"], w=["psNA0"], sig=(g == 3 and n == len(ds_) - 1))
                j = i % 4
                evac(uT[:, :, j * 128:(j + 1) * 128], uv, r=["psNA0"], w=["B_uT"], eng="act")
                if j == 3:
                    par = (i // 4) % 2
                    for p in range(2):
                        mm(PS[BK_NA0][:, :], pwbf[:, p, :], uT[:, p, :], True, True, r=["B_pwbf", "B_uT"], w=["psNA0"])
                        op("act", lambda e, p=p: e.activation(out=pbst[par][:, p, :], in_=PS[BK_NA0][:, :], func=AF.Copy,
                                                              scale=psc[:, p:p + 1]), r=["psNA0", "B_cst"], w=[f"B_pbst{par}"])
                    t0 = (i // 4) * 512
                    op("sp", lambda e: e.dma_start(out=s_pbT[s].rearrange("(c p) t -> p c t", p=128)[:, :, t0:t0 + 512],
                                                   in_=pbst[par][:]), r=[f"B_pbst{par}"], dsem=f"B_pbst{par}")

            def na_gen(s):
                unitsn = [(i, h) for i in range(NB) for h in range(4)]
                onv = PS[BK_NA1][:, 0:260].rearrange("p (h d) -> p h d", h=4)
                s4v = PS[BK_NA1][:, 384:512]

                def scores(i, h):
                    chs = chunks_of(i)
                    pb = (h % 2) * 64
                    qs = slice(i * 128, (i + 1) * 128)
                    for n, j in enumerate(chs):
                        if n < 4:
                            dst, key = PS[BK_NA0][:, n * 128:(n + 1) * 128], KNA0
                        else:
                            dst, key = s4v, KNA1
                        mm(dst, nk[pb:pb + 64, h // 2, j * 128:(j + 1) * 128], nq[pb:pb + 64, h // 2, qs], True, True,
                           r=["B_nk", "B_nq"], w=[key], sig=(n == len(chs) - 1 or n == 3))

                def probs(i, h, k):
                    chs = chunks_of(i)
                    nch = len(chs)
                    tb0 = tile_base(i)
                    pp = k % 2
                    op("act", lambda e: e.activation(out=PT[pp][:, 0:4, :], in_=PS[BK_NA0][:].rearrange("p (a t) -> p a t", a=4),
                                                     func=AF.Exp), r=[KNA0], w=[f"B_PT{pp}", f"B_PTa{pp}", f"B_PTb{pp}"])
                    if nch == 5:
                        op("act", lambda e: e.activation(out=PT[pp][:, 4, :], in_=s4v, func=AF.Exp), r=[KNA1],
                           w=[f"B_PT{pp}", f"B_PTa{pp}", f"B_PTb{pp}"])
                    op("dve", lambda e: e.tensor_tensor(out=PT[pp][:, 0:3, :], in0=PT[pp][:, 0:3, :],
                                                        in1=nab[:, tb0:tb0 + 3, h, :], op=ALU.mult),
                       r=[f"B_PT{pp}", "B_nab"], w=[f"B_PTa{pp}"])
                    op("pool", lambda e: e.tensor_tensor(out=PT[pp][:, 3:nch, :], in0=PT[pp][:, 3:nch, :],
                                                         in1=nab[:, tb0 + 3:tb0 + nch, h, :], op=ALU.mult),
                       r=[f"B_PT{pp}", "B_nab"], w=[f"B_PTb{pp}"])

                def pv(i, h, k):
                    chs = chunks_of(i)
                    pp = k % 2
                    for n, j in enumerate(chs):
                        mm(onv[:, h, 0:65], PT[pp][:, n, :], nvx[:, j, h * 65:(h + 1) * 65], n == 0, n == len(chs) - 1,
                           r=[f"B_PT{pp}", f"B_PTa{pp}", f"B_PTb{pp}", "B_nvx"], w=[KNA1], sig=(n == len(chs) - 1))

                def finish(i):
                    op("dve", lambda e: e.reciprocal(out=rcp[:].rearrange("p (h o) -> p h o", o=1), in_=onv[:, :, 64:65]), r=[KNA1], w=["B_rcp"])
                    for h in range(4):
                        op("dve", lambda e, h=h: e.tensor_scalar(out=na_tm[:, h * 64:(h + 1) * 64], in0=onv[:, h, 0:64],
                                                                 scalar1=rcp[:, h:h + 1], scalar2=None, op0=ALU.mult),
                           r=[KNA1, "B_rcp"], w=["B_natm"])
                    trv = PS[BK_ATO][:].bitcast(BF16)[:, 768:1024].rearrange("p (k t) -> p k t", k=2)
                    for k in range(2):
                        op("pe", lambda e, k=k: e.transpose(trv[:, k, :], na_tm[:, k * 128:(k + 1) * 128], ident[:]),
                           r=["B_natm", "ident"], w=[KATO], sig=(k == 1))
                    j = i % 4
                    par = (i // 4) % 2
                    evac(nast[par][:, :, j * 128:(j + 1) * 128], trv, r=[KATO], w=[f"B_nast{par}"], eng="act")
                    if j == 3:
                        t0 = (i // 4) * 512
                        op("sp", lambda e: e.dma_start(out=s_naT[s].rearrange("(c p) t -> p c t", p=128)[:, :, t0:t0 + 512],
                                                       in_=nast[par][:]), r=[f"B_nast{par}"], dsem=f"B_nast{par}")

                scores(*unitsn[0])
                for k, (i, h) in enumerate(unitsn):
                    probs(i, h, k)
                    if k + 1 < len(unitsn):
                        scores(*unitsn[k + 1])
                    pv(i, h, k)
                    if h == 3:
                        finish(i)
                    yield

            def load_nab():
                for t in range(21):
                    op("pool", lambda e, t=t: e.dma_start(out=nab[:, t], in_=nabias[l][:, t]), w=["B_nab"], dsem="B_nab")

            for s in range(NS):
                g0 = gla_pass(s, 0)
                ev = next(g0)
                seq_loads(s)
                if s == 0:
                    load_nab()
                i = 0
                while ev is not None:
                    if ev == "blk":
                        pool_block(s, i)
                        i += 1
                    ev = next(g0, None)
                seq_loads_bwd(s)
                if s == 0:
                    for t in range(0, 21, 3):
                        op("act", lambda e, t=t: e.activation(out=nab[:, t:t + 3], in_=nab[:, t:t + 3], func=AF.Exp), r=["B_nab"], w=["B_nab"])
                nag = na_gen(s)
                for ev in gla_pass(s, 1):
                    next(nag, None)
                for _ in nag:
                    pass
            S.barrier()

    def phase_C(l, xsrc):
        with ExitStack() as st:
            Wg = sbt(st, "C_wg", [128, 8, 3072], BF16)
            Wa = sbt(st, "C_wa", [128, 4, D], BF16)
            Wb = sbt(st, "C_wb", [128, 2, D], BF16)
            Wc = sbt(st, "C_wc", [128, 2, D], BF16)
            Wo = sbt(st, "C_wo", [128, 8, D], BF16)
            load_w(Wg, w_in[l][:, NMIX:DIN], 8, "C_w", "C_w")
            load_w(Wa, w_bra[l], 4, "C_w", "C_w")
            load_w(Wb, w_brb[l], 2, "C_w", "C_w")
            load_w(Wc, w_brc[l], 2, "C_w", "C_w")
            load_w(Wo, w_out[l], 8, "C_wo", "C_wo")
            lg = sbt(st, "C_lg", [128, D], F32)
            lb = sbt(st, "C_lb", [128, D], F32)
            op("sp", lambda e: e.dma_start(out=lg[:], in_=lnp[l, 0].partition_broadcast(128)), w=["lnp"], dsem="lnp")
            op("sp", lambda e: e.dma_start(out=lb[:], in_=lnp[l, 1].partition_broadcast(128)), w=["lnp"], dsem="lnp")
            xblk = [sbt(st, f"C_x{i}", [128, D], F32) for i in range(8)]
            xbf = [sbt(st, f"C_xbf{i}", [128, D], BF16) for i in range(2)]
            xT = [sbt(st, f"C_xT{i}", [128, 8, 512], BF16) for i in range(2)]
            gT = [sbt(st, f"C_gT{i}", [128, 4, 512], BF16) for i in range(2)]
            pbT = [sbt(st, f"C_pbT{i}", [128, 2, 512], BF16) for i in range(2)]
            naT = [sbt(st, f"C_naT{i}", [128, 2, 512], BF16) for i in range(2)]
            sg = [sbt(st, f"C_sg{i}", [128, 3, 512], F32) for i in range(2)]
            t1 = [sbt(st, f"C_t{i}", [128, 3, 512], F32) for i in range(2)]
            mT = sbt(st, "C_mT", [128, 8, 512], BF16)
            stats = sbt(st, "C_stats", [128, 4, 12], F32)
            mv = sbt(st, "C_mv", [128, 4, 4], F32)
            rstd = sbt(st, "C_rstd", [128, 4, 1], F32)

            def loads(ti):
                par = ti % 2
                s, t0 = divmod(ti, NTL)
                t0 *= 512
                for b in range(4):
                    i = par * 4 + b
                    op("sp", lambda e, i=i, b=b: e.dma_start(out=xblk[i][:], in_=xsrc[ti * 512 + b * 128: ti * 512 + (b + 1) * 128, :]),
                       w=[f"C_x{i}"], dsem=f"C_x{i}")
                for (dst, scr, key) in ((gT, s_gT, "C_gT"), (pbT, s_pbT, "C_pbT"), (naT, s_naT, "C_naT")):
                    op("sp", lambda e, dst=dst, scr=scr: e.dma_start(
                        out=dst[par][:], in_=scr[s].rearrange("(c p) t -> p c t", p=128)[:, :, t0:t0 + 512]),
                       w=[f"{key}{par}"], dsem=f"{key}{par}")

            def mk_xT(ti):
                par = ti % 2
                for b in range(4):
                    make_xT(xblk[par * 4 + b][:], f"C_x{par * 4 + b}", xbf[b % 2], f"C_xbf{b % 2}", xT[par], f"C_xT{par}", b,
                            nextbank(2, 6), ceng="act")

            dfr = Deferred()
            loads(0)
            mk_xT(0)
            if NTILE > 1:
                loads(1)
            for ti in range(NTILE):
                par = ti % 2
                xk = f"C_xT{par}"
                for ch in range(8):
                    q = ch % 2
                    cs = slice(ch * 128, (ch + 1) * 128)
                    for gi in range(3):
                        bk = gi
                        for k in range(8):
                            mm(PS[bk][:], Wg[:, k, gi * D + ch * 128: gi * D + (ch + 1) * 128], xT[par][:, k, :], k == 0, k == 7,
                               r=["C_w", xk], w=[f"ps{bk}"])
                        op("act", lambda e, gi=gi, bk=bk: e.activation(out=sg[q][:, gi, :], in_=PS[bk][:], func=AF.Sigmoid),
                           r=[f"ps{bk}"], w=[f"C_sg{q}"])
                    for gi, (Wx, src, nk_, key) in enumerate(((Wa, gT, 4, "C_gT"), (Wb, pbT, 2, "C_pbT"), (Wc, naT, 2, "C_naT"))):
                        bk = 3 + gi
                        for k in range(nk_):
                            mm(PS[bk][:], Wx[:, k, cs], src[par][:, k, :], k == 0, k == nk_ - 1,
                               r=["C_w", f"{key}{par}"], w=[f"ps{bk}"])
                        op("dve", lambda e, gi=gi, bk=bk: e.tensor_tensor(out=t1[q][:, gi, :], in0=PS[bk][:], in1=sg[q][:, gi, :],
                                                                          op=ALU.mult), r=[f"ps{bk}", f"C_sg{q}"], w=[f"C_t{q}"])
                    op("pool", lambda e: e.tensor_tensor(out=t1[q][:, 0, :], in0=t1[q][:, 0, :], in1=t1[q][:, 1, :], op=ALU.add),
                       r=[f"C_t{q}"], w=[f"C_t{q}"])
                    op("pool", lambda e, ch=ch: e.tensor_tensor(out=mT[:, ch, :], in0=t1[q][:, 0, :], in1=t1[q][:, 2, :], op=ALU.add),
                       r=[f"C_t{q}"], w=["C_mT"])
                    dfr.advance(5)
                dfr.drain()
                if ti + 1 < NTILE:
                    mk_xT(ti + 1)
                for b in range(4):
                    xi = par * 4 + b
                    xkey = f"C_x{xi}"
                    for n in range(2):
                        bk = nextbank(6, 0)
                        for k in range(8):
                            mm(PS[bk][:], mT[:, k, b * 128:(b + 1) * 128], Wo[:, k, n * 512:(n + 1) * 512], k == 0, k == 7,
                               r=["C_wo", "C_mT"], w=[f"ps{bk}"])
                        op("dve", lambda e, n=n, bk=bk, xi=xi: e.scalar_tensor_tensor(
                            out=xblk[xi][:, n * 512:(n + 1) * 512], in0=xblk[xi][:, n * 512:(n + 1) * 512], scalar=ALPHA,
                            in1=PS[bk][:], op0=ALU.mult, op1=ALU.add), r=[xkey, f"ps{bk}"], w=[xkey])

                def ln_store(b, par=par, ti=ti):
                    xi = par * 4 + b
                    xkey = f"C_x{xi}"
                    yield from layer_norm(st, xblk[xi][:], xkey, lg, lb, f"C_{b}", stats[:, b], mv[:, b], rstd[:, b])
                    op("sp", lambda e: e.dma_start(out=X1[ti * 512 + b * 128: ti * 512 + (b + 1) * 128, :], in_=xblk[xi][:]),
                       r=[xkey], dsem=f"C_xs{xi}")
                nxt_loads = (lambda t=ti + 2: loads(t)) if ti + 2 < NTILE else None
                dfr.set([ln_store(b) for b in range(4)], after=nxt_loads)
            dfr.drain()
            S.barrier()

    def phase_D(l, xdst):
        NJ = DFF // 128
        with ExitStack() as st:
            Wg = sbt(st, "D_wg", [128, 8, DFF], BF16)
            Wu = sbt(st, "D_wu", [128, 8, DFF], BF16)
            Wd = sbt(st, "D_wd", [128, NJ, D], BF16)
            GJ = [(0, 6), (6, 14), (14, NJ)]
            jgrp = {}
            for g, (j0, j1) in enumerate(GJ):
                for j in range(j0, j1):
                    jgrp[j] = g
                for (Wx, wsrc, nm) in ((Wg, w_gate, "g"), (Wu, w_up, "u")):
                    for k in range(8):
                        op("pool", lambda e, Wx=Wx, wsrc=wsrc, k=k, j0=j0, j1=j1: e.dma_start(
                            out=Wx[:, k, j0 * 128:j1 * 128], in_=wsrc[l][k * 128:(k + 1) * 128, j0 * 128:j1 * 128]),
                           w=[f"D_wgu{g}"], dsem=f"D_wgu{g}")
            for g, (j0, j1) in enumerate(GJ):
                for j in range(j0, j1):
                    op("pool", lambda e, j=j: e.dma_start(out=Wd[:, j, :], in_=w_down[l][j * 128:(j + 1) * 128, :]),
                       w=["D_wd"], dsem="D_wd")
            lg = sbt(st, "D_lg", [128, D], F32)
            lb = sbt(st, "D_lb", [128, D], F32)
            op("sp", lambda e: e.dma_start(out=lg[:], in_=lnp[l, 2].partition_broadcast(128)), w=["lnp"], dsem="lnp")
            op("sp", lambda e: e.dma_start(out=lb[:], in_=lnp[l, 3].partition_broadcast(128)), w=["lnp"], dsem="lnp")
            xs = [sbt(st, f"D_xs{i}", [128, D], F32) for i in range(2)]
            xr = [sbt(st, f"D_xr{i}", [128, D], F32) for i in range(4)]
            xbf = sbt(st, "D_xbf", [128, D], BF16)
            xT = sbt(st, "D_xT", [128, 8, 512], BF16)
            hT = sbt(st, "D_hT", [128, NJ, 512], BF16)
            sgs = [sbt(st, f"D_sg{i}", [128, 512], F32) for i in range(2)]
            stats = sbt(st, "D_stats", [128, 4, 12], F32)
            mv = sbt(st, "D_mv", [128, 4, 4], F32)
            rstd = sbt(st, "D_rstd", [128, 4, 1], F32)

            def rows(ti, b_):
                return slice(ti * 512 + b_ * 128, ti * 512 + (b_ + 1) * 128)

            def load_xs(ti, b_):
                i = b_ % 2
                op("sp", lambda e: e.dma_start(out=xs[i][:], in_=X1[rows(ti, b_), :]), w=[f"D_xs{i}"], dsem=f"D_xs{i}")

            def load_xr(ti):
                for b_ in range(4):
                    op("sp", lambda e, b_=b_: e.dma_start(out=xr[b_][:], in_=X1[rows(ti, b_), :]), w=[f"D_xr{b_}"], dsem=f"D_xr{b_}")

            def mk_xT(ti):
                for b_ in range(4):
                    make_xT(xs[b_ % 2][:], f"D_xs{b_ % 2}", xbf, "D_xbf", xT, "D_xT", b_, nextbank(2, 6), ceng="act")
                    if b_ + 2 < 4:
                        load_xs(ti, b_ + 2)

            dfr = Deferred()
            load_xs(0, 0)
            load_xs(0, 1)
            mk_xT(0)
            load_xr(0)
            for ti in range(NTILE):
                if ti + 1 < NTILE:
                    load_xs(ti + 1, 0)
                    load_xs(ti + 1, 1)
                for j in range(NJ):
                    q = j % 2
                    cs = slice(j * 128, (j + 1) * 128)
                    bg = 0 + q
                    bu = 2 + q
                    for k in range(8):
                        mm(PS[bg][:], Wg[:, k, cs], xT[:, k, :], k == 0, k == 7, r=[f"D_wgu{jgrp[j]}", "D_xT"], w=[f"ps{bg}"])
                    for k in range(8):
                        mm(PS[bu][:], Wu[:, k, cs], xT[:, k, :], k == 0, k == 7, r=[f"D_wgu{jgrp[j]}", "D_xT"], w=[f"ps{bu}"])
                    op("act", lambda e, q=q, bg=bg: e.activation(out=sgs[q][:], in_=PS[bg][:], func=AF.Silu), r=[f"ps{bg}"], w=[f"D_sg{q}"])
                    op("dve", lambda e, q=q, bu=bu, j=j: e.tensor_tensor(out=hT[:, j, :], in0=PS[bu][:], in1=sgs[q][:], op=ALU.mult),
                       r=[f"ps{bu}", f"D_sg{q}"], w=["D_hT"])
                    dfr.advance(2)
                dfr.drain()
                if ti + 1 < NTILE:
                    mk_xT(ti + 1)
                for b_ in range(4):
                    xkey = f"D_xr{b_}"
                    for n in range(2):
                        bk = nextbank(6, 0)
                        for j in range(NJ):
                            mm(PS[bk][:], hT[:, j, b_ * 128:(b_ + 1) * 128], Wd[:, j, n * 512:(n + 1) * 512], j == 0, j == NJ - 1,
                               r=["D_wd", "D_hT"], w=[f"ps{bk}"])
                        op("dve", lambda e, n=n, bk=bk, b_=b_: e.scalar_tensor_tensor(
                            out=xr[b_][:, n * 512:(n + 1) * 512], in0=xr[b_][:, n * 512:(n + 1) * 512], scalar=ALPHA,
                            in1=PS[bk][:], op0=ALU.mult, op1=ALU.add), r=[xkey, f"ps{bk}"], w=[xkey])

                def ln_store(b_, ti=ti):
                    xkey = f"D_xr{b_}"
                    yield from layer_norm(st, xr[b_][:], xkey, lg, lb, f"D_{b_}", stats[:, b_], mv[:, b_], rstd[:, b_])
                    op("sp", lambda e: e.dma_start(out=xdst[rows(ti, b_), :], in_=xr[b_][:]), r=[xkey], dsem=f"D_xo{b_}")
                nxt = (lambda t=ti + 1: load_xr(t)) if ti + 1 < NTILE else None
                dfr.set([ln_store(b_) for b_ in range(4)], after=nxt)
            dfr.drain()
            S.barrier()

    S.barrier()
    for l in range(L):
        xsrc = x_in if l == 0 else X2
        phase_A(l, xsrc)
        phase_B(l)
        phase_C(l, xsrc)
        phase_D(l, y_out if l == L - 1 else X2)
    glob.close()
    return nc, S


def host_inputs(T, L, w):
    NB = T // 128
    c = _consts(T)
    ri, ci, va = _na_bias_index(NB)
    rpb = np.asarray(w["na_rpb"], np.float32)
    nab = rpb[:, :, ri, ci]
    nab = np.where(va[None, None], nab, np.float32(NEG))
    nab = np.ascontiguousarray(nab.transpose(0, 3, 2, 1, 4))
    upx = np.zeros((L, 2, 17, 256), np.float32)
    upx[:, 0, :16] = w["gla_up_f"]
    upx[:, 1, :16] = w["gla_up_b"]
    upx[:, 0, 16] = w["gla_bias_f"]
    upx[:, 1, 16] = w["gla_bias_b"]
    lnp = np.stack([w["ln1_g"], w["ln1_b"], w["ln2_g"], w["ln2_b"]], axis=1).astype(np.float32)
    m = {
        "w_in": w["w_in"], "w_br_a": w["w_br_a"], "w_br_b": w["w_br_b"], "w_br_c": w["w_br_c"], "w_out": w["w_out"],
        "w_gate": w["w_gate"], "w_up": w["w_up"], "w_down": w["w_down"], "upx": upx, "gla_norm": w["gla_norm"],
        "pool_w": w["pool_w"], "pool_scale": np.asarray(w["pool_scale"]).reshape(L, 2, 128).transpose(0, 2, 1), "nabias": nab, "lnp": lnp,
        "ident": c["ident"], "tri": np.stack([c["trix_f"], c["tris_f"], c["trix_b"], c["tris_b"]]),
        "mask": np.stack([c["mask_f"], c["mask_b"]]), "bmat": c["bmat"], "negh": c["negh"],
    }
    return {k: np.ascontiguousarray(np.asarray(v, np.float32)) for k, v in m.items()}


def kernel(**inputs):
    L, T, NS, NCORE = 4, 4096, 3, 8
    xp = np.asarray(inputs["x_prompt"], np.float32)
    xs = np.asarray(inputs["x_sample"], np.float32)
    seqs = [xp[i] for i in range(xp.shape[0])] + [xs[i] for i in range(xs.shape[0])]
    nseq = len(seqs)
    assign = [[(c + 8 * j) % nseq if (c + 8 * j) < nseq else c for j in range(NS)] for c in range(NCORE)]
    shared = host_inputs(T, L, inputs)
    nc, _ = build(NS, T, L)
    in_maps = []
    for c in range(NCORE):
        m = dict(shared)
        m["x"] = np.ascontiguousarray(np.concatenate([seqs[i] for i in assign[c]], axis=0))
        in_maps.append(m)
    res = run_bass_kernel_spmd(nc, in_maps, core_ids=list(range(NCORE)))
    outs = [None] * nseq
    for c in range(NCORE):
        y = np.asarray(res.results[c]["y"], np.float32).reshape(NS, T, D)
        for j in range(NS):
            if c + 8 * j < nseq:
                outs[c + 8 * j] = y[j]
    y_prompt = np.stack(outs[:xp.shape[0]], axis=0)
    y_sample = np.stack(outs[xp.shape[0]:], axis=0)
    return (y_prompt, y_sample)
```
